# Optimizing a Trainium2 kernel written in Bass

```python
import math, functools
import jax, jax.numpy as jnp
from jax import lax
import numpy as np

D_MODEL = 1024
BATCH = 8
SEQ = 4096
DEPTH = 1
DEC_BATCH = 16
DEC_SEQ = 16
PAST_LEN = 2048

CHUNK = 64
N_Q_HEADS = 16
N_KV_HEADS = 2
HEAD_DIM = 64
Q_PER_KV = N_Q_HEADS // N_KV_HEADS
ATTN_WIDTH = N_Q_HEADS * HEAD_DIM
KV_WIDTH = N_KV_HEADS * HEAD_DIM
WINDOW = 128
WINDOW_CHUNKS = WINDOW // CHUNK
GMLP_WIDTH = 1024
GMLP_GROUPS = 4
GMLP_GROUP_DIM = GMLP_WIDTH // GMLP_GROUPS
GMLP_CHUNK = 128
NUM_BUCKETS = 32
MAX_DISTANCE = 128
D_FF = 2816
EPS = 1e-6
NEG_INF = -1e30
IN_WIDTH = ATTN_WIDTH + 2 * KV_WIDTH + 2 * GMLP_WIDTH + 2 * D_MODEL
SPLIT_POINTS = (ATTN_WIDTH,
                ATTN_WIDTH + KV_WIDTH,
                ATTN_WIDTH + 2 * KV_WIDTH,
                ATTN_WIDTH + 2 * KV_WIDTH + GMLP_WIDTH,
                ATTN_WIDTH + 2 * KV_WIDTH + 2 * GMLP_WIDTH,
                ATTN_WIDTH + 2 * KV_WIDTH + 2 * GMLP_WIDTH + D_MODEL)

kernel_name = 'hybrid_swa_gmlp_streaming_step'


def rms_norm(x, g):
    xf = x.astype(jnp.float32)
    y = xf * lax.rsqrt(jnp.mean(xf * xf, axis=-1, keepdims=True) + EPS)
    return (y * g.astype(jnp.float32)).astype(x.dtype)


def layer_norm(x, g, b):
    xf = x.astype(jnp.float32)
    mu = jnp.mean(xf, axis=-1, keepdims=True)
    xc = xf - mu
    var = jnp.mean(xc * xc, axis=-1, keepdims=True)
    return (xc * lax.rsqrt(var + EPS) * g.astype(jnp.float32) + b.astype(jnp.float32)).astype(x.dtype)


def swiglu(x, w_gate, w_up, w_down):
    return (jax.nn.silu(x @ w_gate) * (x @ w_up)) @ w_down


def t5_bucket(rel):
    half = NUM_BUCKETS // 2
    max_exact = half // 2
    ret = jnp.where(rel > 0, half, 0)
    n = jnp.abs(rel)
    nf = jnp.maximum(n, 1).astype(jnp.float32)
    large = max_exact + (jnp.log(nf / max_exact) / math.log(MAX_DISTANCE / max_exact)
                         * (half - max_exact)).astype(jnp.int32)
    large = jnp.minimum(large, half - 1)
    return ret + jnp.where(n < max_exact, n, large)


def relative_bias(table, q_pos, k_pos):
    rel = k_pos[None, :] - q_pos[:, None]
    b = table[t5_bucket(rel)].astype(jnp.float32)
    b = b.reshape(q_pos.shape[0], k_pos.shape[0], N_KV_HEADS, Q_PER_KV)
    return jnp.transpose(b, (2, 3, 0, 1))


def sink_attention(q, k, v, bias, sinks, mask=None):
    s = jnp.einsum('...qhgd,...khd->...hgqk', q, k).astype(jnp.float32) * (HEAD_DIM ** -0.5) + bias
    if mask is not None:
        s = jnp.where(mask, s, NEG_INF)
    sink = sinks.astype(jnp.float32).reshape(N_KV_HEADS, Q_PER_KV)[:, :, None, None]
    sink_col = jnp.broadcast_to(sink, s.shape[:-1] + (1,))
    p = jax.nn.softmax(jnp.concatenate([s, sink_col], axis=-1), axis=-1)[..., :-1]
    return jnp.einsum('...hgqk,...khd->...qhgd', p.astype(v.dtype), v)


def window_attention_prompt(q, k, v, rel_table, sinks):
    B, S = q.shape[0], q.shape[1]
    n_chunks = S // CHUNK
    n_keys = (WINDOW_CHUNKS + 1) * CHUNK
    qb = q.reshape(B, n_chunks, CHUNK, N_KV_HEADS, Q_PER_KV, HEAD_DIM)

    def key_blocks(t):
        tp = jnp.pad(t.reshape(B, S, N_KV_HEADS, HEAD_DIM),
                     ((0, 0), (WINDOW_CHUNKS * CHUNK, 0), (0, 0), (0, 0)))
        tp = tp.reshape(B, n_chunks + WINDOW_CHUNKS, CHUNK, N_KV_HEADS, HEAD_DIM)
        return jnp.concatenate([tp[:, i:i + n_chunks] for i in range(WINDOW_CHUNKS + 1)], axis=2)

    kb, vb = key_blocks(k), key_blocks(v)
    key_chunk = jnp.arange(n_chunks)[:, None] + jnp.arange(n_keys)[None, :] // CHUNK - WINDOW_CHUNKS
    mask = (key_chunk >= 0)[None, :, None, None, None, :]
    bias = relative_bias(rel_table, jnp.arange(CHUNK), jnp.arange(n_keys) - WINDOW_CHUNKS * CHUNK)
    o = sink_attention(qb, kb, vb, bias, sinks, mask)
    return o.reshape(B, S, ATTN_WIDTH)


def spatial_gate(u, vn, w_s, b_s):
    L = u.shape[2]
    blk = jnp.arange(L) // CHUNK
    mask = blk[None, :] <= blk[:, None]
    w = jnp.where(mask[None], w_s[:, :L, :L], 0).astype(vn.dtype)
    bias = jnp.transpose(b_s[:, :L])[:, :, None].astype(vn.dtype)
    return u * (jnp.einsum('gij,bcjgd->bcigd', w, vn) + bias)


def prompt_mixer(q, k, v, u, vn, rel_table, sinks, w_s, b_s):
    B, S = q.shape[0], q.shape[1]
    attn = window_attention_prompt(q, k, v, rel_table, sinks)
    nb = S // GMLP_CHUNK
    shp = (B, nb, GMLP_CHUNK, GMLP_GROUPS, GMLP_GROUP_DIM)
    gm = spatial_gate(u.reshape(shp), vn.reshape(shp), w_s, b_s).reshape(B, S, GMLP_WIDTH)
    kw = k.reshape(B, S, N_KV_HEADS, HEAD_DIM)[:, S - WINDOW:]
    vw = v.reshape(B, S, N_KV_HEADS, HEAD_DIM)[:, S - WINDOW:]
    return attn, gm, (kw, vw)


def sample_mixer(q, k, v, u, vn, cache_k, cache_v, rel_table, sinks, w_s, b_s):
    Bd, Q = q.shape[0], q.shape[1]
    qh = q.reshape(Bd, Q, N_KV_HEADS, Q_PER_KV, HEAD_DIM)
    kh = k.reshape(Bd, Q, N_KV_HEADS, HEAD_DIM)
    vh = v.reshape(Bd, Q, N_KV_HEADS, HEAD_DIM)
    n_cache = cache_k.shape[1]
    k_all = jnp.concatenate([cache_k.astype(kh.dtype), kh], axis=1)
    v_all = jnp.concatenate([cache_v.astype(vh.dtype), vh], axis=1)
    bias = relative_bias(rel_table, jnp.arange(Q), jnp.arange(n_cache + Q) - n_cache)
    attn = sink_attention(qh, k_all, v_all, bias, sinks).reshape(Bd, Q, ATTN_WIDTH)
    shp = (Bd, 1, Q, GMLP_GROUPS, GMLP_GROUP_DIM)
    gm = spatial_gate(u.reshape(shp), vn.reshape(shp), w_s, b_s).reshape(Bd, Q, GMLP_WIDTH)
    return attn, gm, (kh, vh, vn)


def layer_forward(x, mixer, norm_g, f1_g, f1_u, f1_d, w_in, ln_g, ln_b,
                  w_ba, w_bg, w_o, f2_g, f2_u, f2_d):
    x = x + 0.5 * rms_norm(swiglu(rms_norm(x, norm_g[0]), f1_g, f1_u, f1_d), norm_g[1])
    h = rms_norm(x, norm_g[2])
    q, k, v, u, gv, ga, gb = jnp.split(h @ w_in, SPLIT_POINTS, axis=-1)
    u = jax.nn.gelu(u, approximate=False)
    vn = layer_norm(jax.nn.gelu(gv, approximate=False), ln_g, ln_b)
    attn, gm, state = mixer(q, k, v, u, vn)
    merged = jax.nn.sigmoid(ga) * (attn @ w_ba) + jax.nn.sigmoid(gb) * (gm @ w_bg)
    x = x + rms_norm(merged @ w_o, norm_g[3])
    x = x + 0.5 * rms_norm(swiglu(rms_norm(x, norm_g[4]), f2_g, f2_u, f2_d), norm_g[5])
    return x, state


def setup_inputs(seed: int = 0) -> dict:
    key = jax.random.key(seed)
    ks = jax.random.split(key, 24)
    f32 = jnp.float32

    def nrm(k, shape, scale):
        return jax.random.normal(k, shape, f32) * scale

    n_cache = min(WINDOW, PAST_LEN)
    return {
        'x_prompt': nrm(ks[0], (BATCH, SEQ, D_MODEL), 1.0),
        'x_sample': nrm(ks[1], (DEC_BATCH, DEC_SEQ, D_MODEL), 1.0),
        'cache_win_k': nrm(ks[2], (DEPTH, DEC_BATCH, n_cache, N_KV_HEADS, HEAD_DIM), 1.0),
        'cache_win_v': nrm(ks[3], (DEPTH, DEC_BATCH, n_cache, N_KV_HEADS, HEAD_DIM), 1.0),
        'rel_bias_table': nrm(ks[4], (NUM_BUCKETS, N_Q_HEADS), 0.5),
        'norm_gains': 1.0 + nrm(ks[5], (DEPTH, 6, D_MODEL), 0.05),
        'ffn1_w_gate': nrm(ks[6], (DEPTH, D_MODEL, D_FF), D_MODEL ** -0.5),
        'ffn1_w_up': nrm(ks[7], (DEPTH, D_MODEL, D_FF), D_MODEL ** -0.5),
        'ffn1_w_down': nrm(ks[8], (DEPTH, D_FF, D_MODEL), D_FF ** -0.5),
        'w_in': nrm(ks[9], (DEPTH, D_MODEL, IN_WIDTH), D_MODEL ** -0.5),
        'attn_sinks': nrm(ks[10], (DEPTH, N_Q_HEADS), 0.5),
        'gmlp_ln_g': 1.0 + nrm(ks[11], (DEPTH, GMLP_WIDTH), 0.05),
        'gmlp_ln_b': nrm(ks[12], (DEPTH, GMLP_WIDTH), 0.02),
        'gmlp_w_s': nrm(ks[13], (DEPTH, GMLP_GROUPS, GMLP_CHUNK, GMLP_CHUNK), GMLP_CHUNK ** -0.5),
        'gmlp_b_s': 1.0 + nrm(ks[14], (DEPTH, GMLP_GROUPS, GMLP_CHUNK), 0.05),
        'w_branch_attn': nrm(ks[15], (DEPTH, ATTN_WIDTH, D_MODEL), ATTN_WIDTH ** -0.5),
        'w_branch_gmlp': nrm(ks[16], (DEPTH, GMLP_WIDTH, D_MODEL), GMLP_WIDTH ** -0.5),
        'w_out': nrm(ks[17], (DEPTH, D_MODEL, D_MODEL), D_MODEL ** -0.5),
        'ffn2_w_gate': nrm(ks[18], (DEPTH, D_MODEL, D_FF), D_MODEL ** -0.5),
        'ffn2_w_up': nrm(ks[19], (DEPTH, D_MODEL, D_FF), D_MODEL ** -0.5),
        'ffn2_w_down': nrm(ks[20], (DEPTH, D_FF, D_MODEL), D_FF ** -0.5),
    }


def reference(x_prompt, x_sample, cache_win_k, cache_win_v, rel_bias_table, norm_gains,
              ffn1_w_gate, ffn1_w_up, ffn1_w_down, w_in, attn_sinks, gmlp_ln_g, gmlp_ln_b,
              gmlp_w_s, gmlp_b_s, w_branch_attn, w_branch_gmlp, w_out,
              ffn2_w_gate, ffn2_w_up, ffn2_w_down):
    xp, xs = x_prompt, x_sample
    kp_list, vp_list, ks_list, vs_list, gs_list = [], [], [], [], []
    for l in range(DEPTH):
        shared = (norm_gains[l], ffn1_w_gate[l], ffn1_w_up[l], ffn1_w_down[l], w_in[l],
                  gmlp_ln_g[l], gmlp_ln_b[l], w_branch_attn[l], w_branch_gmlp[l], w_out[l],
                  ffn2_w_gate[l], ffn2_w_up[l], ffn2_w_down[l])
        p_mix = functools.partial(prompt_mixer, rel_table=rel_bias_table, sinks=attn_sinks[l],
                                  w_s=gmlp_w_s[l], b_s=gmlp_b_s[l])
        s_mix = functools.partial(sample_mixer, cache_k=cache_win_k[l], cache_v=cache_win_v[l],
                                  rel_table=rel_bias_table, sinks=attn_sinks[l],
                                  w_s=gmlp_w_s[l], b_s=gmlp_b_s[l])
        xp, (kw, vw) = layer_forward(xp, p_mix, *shared)
        xs, (kn, vn_rows, gv_rows) = layer_forward(xs, s_mix, *shared)
        kp_list.append(kw)
        vp_list.append(vw)
        ks_list.append(kn)
        vs_list.append(vn_rows)
        gs_list.append(gv_rows)
    win_k_prompt = jnp.stack(kp_list, axis=0)
    win_v_prompt = jnp.stack(vp_list, axis=0)
    new_k_sample = jnp.stack(ks_list, axis=0)
    new_v_sample = jnp.stack(vs_list, axis=0)
    gmlp_v_sample = jnp.stack(gs_list, axis=0)
    return (xp, xs, win_k_prompt, win_v_prompt, new_k_sample, new_v_sample, gmlp_v_sample)
```

```python
import math
import types
from contextlib import ExitStack

import numpy as np
import concourse.bass as bass
import concourse.mybir as mybir
from concourse.bass_utils import run_bass_kernel_spmd

F32 = mybir.dt.float32
BF16 = mybir.dt.bfloat16
I32 = mybir.dt.int32
AF = mybir.ActivationFunctionType
ALU = mybir.AluOpType

D = 1024
DFF = 2816
NFC = 22
INW = 5376
SEQ = 4096
TT = 512
NPT = SEQ // TT
EPS = 1e-6
NEG = -30000.0
NSLOT = 7
NCONV = 8
MAGIC = float(0x5F3759DF)


class Ev:
    __slots__ = ("sem", "val")

    def __init__(self, sem, val):
        self.sem = sem
        self.val = val


class DmaSem:
    def __init__(self, key):
        self.key = key
        self.count = 0


class Tracker:
    def __init__(self):
        self.lw = {}
        self.rd = {}


def freeze(fn):
    if fn.__closure__ is None:
        return fn
    cells = []
    for c in fn.__closure__:
        try:
            cells.append(types.CellType(c.cell_contents))
        except ValueError:
            cells.append(c)
    return types.FunctionType(fn.__code__, fn.__globals__, fn.__name__, fn.__defaults__, tuple(cells))


class Eng:
    def __init__(self, name, tr, sync_self=True):
        self.name = name
        self.tr = tr
        self.items = []
        self.count = 0
        self.waited = {}
        self.sync_self = sync_self

    def emit(self, fn, reads=(), writes=(), dma=None):
        tr = self.tr
        deps = {}

        def add(ev):
            if ev is not None and deps.get(ev.sem, 0) < ev.val:
                deps[ev.sem] = ev.val

        for c in reads:
            add(tr.lw.get(c))
        for c in writes:
            add(tr.lw.get(c))
            for ev in tr.rd.get(c, {}).values():
                add(ev)
        for k, v in deps.items():
            if k == self.name and not self.sync_self:
                continue
            if self.waited.get(k, 0) < v:
                self.items.append(("w", k, v))
                self.waited[k] = v
        if dma is None:
            self.count += 1
            ev = Ev(self.name, self.count)
            inc = (self.name, 1)
        else:
            dma.count += 16
            ev = Ev(dma.key, dma.count)
            inc = (dma.key, 16)
        self.items.append(("o", freeze(fn), inc))
        for c in reads:
            tr.rd.setdefault(c, {})[ev.sem] = ev
        for c in writes:
            tr.lw[c] = ev
            tr.rd[c] = {}
        return ev

    def wait_ev(self, ev):
        if self.waited.get(ev.sem, 0) < ev.val:
            self.items.append(("w", ev.sem, ev.val))
            self.waited[ev.sem] = ev.val


def t5_bucket_np(rel):
    half = 16
    max_exact = 8
    ret = np.where(rel > 0, half, 0)
    n = np.abs(rel)
    nf = np.maximum(n, 1).astype(np.float32)
    large = max_exact + (np.log(nf / np.float32(max_exact)) / np.float32(math.log(128 / max_exact))
                         * np.float32(half - max_exact)).astype(np.int32)
    large = np.minimum(large, half - 1)
    return ret + np.where(n < max_exact, n, large)


def piece_list():
    pcs = []
    for f in (1, 2):
        ffn = []
        for j in range(11):
            ffn.append(("g%d" % f, j))
            ffn.append(("u%d" % f, j))
        for j in range(11):
            ffn.append(("d%d" % f, j))
        if f == 1:
            pcs += ffn
            pcs.append(("kv", 0))
            for nm in ("q", "gv", "u"):
                for j in range(4):
                    pcs.append((nm, j))
            for j in range(4):
                pcs.append(("ga", j))
                pcs.append(("gb", j))
            for j in range(4):
                pcs.append(("ba", j))
                pcs.append(("bg", j))
            for j in range(4):
                pcs.append(("o", j))
        else:
            pcs += ffn
    return pcs


PIECES = piece_list()
NPIECE = len(PIECES)


def build_program():
    nc = bass.Bass("TRN2", target_bir_lowering=False)

    def din(name, shape, dt=F32):
        return nc.dram_tensor(name, list(shape), dt, kind="ExternalInput").ap()

    def dout(name, shape, dt=F32):
        return nc.dram_tensor(name, list(shape), dt, kind="ExternalOutput").ap()

    xp = din("xp", [SEQ, D])
    xs = din("xs", [32, D])
    ck = din("ck", [2, 128, 128])
    cv = din("cv", [2, 128, 128])
    tbl = din("tbl", [32, 16])
    gains = din("gains", [6, D])
    W = {
        "g1": din("f1g", [D, DFF]), "u1": din("f1u", [D, DFF]), "d1": din("f1d", [DFF, D]),
        "g2": din("f2g", [D, DFF]), "u2": din("f2u", [D, DFF]), "d2": din("f2d", [DFF, D]),
        "win": din("win", [D, INW]), "ba": din("wba", [D, D]), "bg": din("wbg", [D, D]), "o": din("wo", [D, D]),
    }
    sinks = din("sinks", [1, 16])
    lng = din("lng", [1, D])
    lnb = din("lnb", [1, D])
    wsd = din("wsd", [4, 128, 128])
    bsd = din("bsd", [1, 512])
    identd = din("identd", [128, 128])
    ohd = din("ohd", [32, 384])
    jd = din("jd", [128, 192])

    yp = dout("yp", [SEQ, D])
    ys = dout("ys", [32, D])
    wkp = dout("wkp", [128, 128])
    wvp = dout("wvp", [128, 128])
    nks = dout("nks", [32, 128])
    nvs = dout("nvs", [32, 128])
    gvs = dout("gvs", [32, D])

    wsc = nc.dram_tensor("wsc", [NPIECE, 128, 2048], BF16, kind="Internal").ap()
    tsc = nc.dram_tensor("tsc", [16, 384], F32, kind="Internal").ap()

    tr = Tracker()
    PE = Eng("PE", tr, sync_self=False)
    ACT = Eng("ACT", tr)
    DVE = Eng("DVE", tr)
    POOL = Eng("POOL", tr)
    SP = Eng("SP", tr)
    dsem = {}

    def ds(key):
        if key not in dsem:
            dsem[key] = DmaSem(key)
        return dsem[key]

    with ExitStack() as es:
        def sb(name, shape, dt):
            return es.enter_context(nc.sbuf_tensor(name, list(shape), dt))

        ps = es.enter_context(nc.psum_tensor("ps", [128, 4096], F32))

        identf = sb("identf", [128, 128], F32)
        identb = sb("identb", [128, 128], BF16)
        gb = sb("gb", [128, 3, D], F32)
        gcol = sb("gcol", [128, 3, 8], F32)
        lngb = sb("lngb", [128, D], F32)
        lnbb = sb("lnbb", [128, D], F32)
        bsb = sb("bsb", [128, 4, 128], F32)
        esink = sb("esink", [128, 16], F32)
        wsT = sb("wsT", [128, 4, 128], BF16)
        wsTs = sb("wsTs", [64, 4, 16], BF16)
        kTc = sb("kTc", [128, 2, 2, 128], BF16)
        Vc = sb("Vc", [128, 2, 2, 65], BF16)
        biasHL = sb("biasHL", [128, 2, 2, 16, 128], BF16)
        biasSHL = sb("biasSHL", [64, 2, 16, 16], BF16)
        xbuf = sb("xbuf", [128, 2, 4, D], F32)
        xsb = sb("xsb", [64, D], F32)
        hTs = [sb("hT0", [128, 8, TT], BF16), sb("hT1", [128, 8, TT], BF16), sb("hTS", [128, 8, 64], BF16)]
        hbuf = sb("hbuf", [128, 4, D], BF16)
        ring = sb("ring", [128, NSLOT, 2048], BF16)
        kT = sb("kT", [128, 2, 640], BF16)
        kTs = sb("kTs", [128, 2, 64], BF16)
        Vaug = sb("Vaug", [128, 5, 2, 65], BF16)
        Vnew = sb("Vnew", [64, 2, 65], BF16)
        kvst = sb("kvst", [128, 256], F32)
        bufABC = sb("bufABC", [128, 24, TT], BF16)
        bufA = bufABC[:, 0:8, :]
        bufB = bufABC[:, 8:16, :]
        bufC = bufABC[:, 16:24, :]
        act = bufABC[:, 0:NFC, :]
        bufD = sb("bufD", [128, 4, D], BF16)
        bufE = sb("bufE", [128, 8, TT], BF16)
        scr = bufD[:].rearrange("p a b -> p (a b)").bitcast(F32)
        oh_sb = scr[0:32, 0:384]
        tst = scr[0:16, 384:768]
        ws_sb = scr[:, 768:1280].rearrange("p (g j) -> p g j", g=4)
        ck_sb = scr[:, 1280:1536].rearrange("p (s f) -> p s f", s=2)
        cv_sb = scr[:, 1536:1792].rearrange("p (s f) -> p s f", s=2)
        jsb = scr[:, 1792:1984]
        tbl_sb = scr[0:32, 1984:2000]
        wsTs_f = sb("wsTs_f", [64, 4, 16], F32)
        XR = [("bufD", t_) for t_ in range(4)]
        PT = sb("PT", [128, 2, 2, 4, 128], BF16)
        obuf = sb("obuf", [128, 2, 16, 64], BF16)
        stmp = sb("stmp", [128, 4, 512], F32)
        lnp = sb("lnp", [128, 4, 2], F32)
        mv4 = sb("mv4", [128, 4, 2], F32)
        sgt = sb("sgt", [128, 2, 512], F32)
        tmpA = sb("tmpA", [128, D], F32)
        tmpB = sb("tmpB", [128, D], F32)
        junk = tmpB[:].bitcast(BF16)[:, 0:D]
        ssq = sb("ssq", [128, 4], F32)
        rstd = sb("rstd", [128, 4], F32)
        nwv = sb("nwv", [128, 4], F32)
        nwi = sb("nwi", [128, 4], I32)
        nwy = sb("nwy", [128, 4], I32)
        nwt = sb("nwt", [128, 4], F32)
        bnst = sb("bnst", [128, 2, 6], F32)
        mv = sb("mv", [128, 2], F32)
        den = sb("den", [128, 4], F32)
        rden = sb("rden", [128, 4], F32)

        qT = sgaT = bufA
        AT = bufB
        uT = sgbT = bufC
        gmT = bufE

        def bank(b):
            return ps[:, b * 512:(b + 1) * 512]

        def bankbf(b):
            return ps[:, b * 512:(b + 1) * 512].bitcast(BF16)

        def pcell(b, q0=0, q1=4):
            return [("ps", b)]

        init_evs = []
        dinit = ds("dinit")

        def init_load(out_ap, in_ap, cells, slow=False):
            def fn(e, o=out_ap, i=in_ap, s=slow):
                if s:
                    return e.dma_start(out=o, in_=i, allow_slow_non_contiguous=True)
                return e.dma_start(out=o, in_=i)
            ev = SP.emit(fn, reads=(), writes=cells, dma=dinit)
            init_evs.append(ev)

        init_load(identf[:], identd, ["identf"])
        init_load(gb[:], bass.AP(tensor=gains.tensor, offset=D, ap=[[0, 128], [2 * D, 3], [1, D]]), ["gb"])
        for i3 in range(3):
            init_load(gcol[:, i3, :], gains[2 * i3, :].rearrange("(kc p) -> p kc", p=128), ["gcol"] if i3 == 0 else [("gcolx", i3)], slow=True)
        init_load(lngb[:], lng.to_broadcast([128, D]), ["lngb"])
        init_load(lnbb[:], lnb.to_broadcast([128, D]), ["lnbb"])
        init_load(bsb[:].rearrange("p g i -> p (g i)"), bsd.to_broadcast([128, 512]), ["bsb"])
        init_load(esink[:], sinks.to_broadcast([128, 16]), ["esink"])
        init_load(tbl_sb[:], tbl, ["tbl_sb"])
        init_load(oh_sb[:], ohd, ["oh_sb"])
        init_load(jsb[:], jd, ["jsb"])
        init_load(ws_sb[:], wsd.rearrange("g i j -> i g j"), ["ws_sb"])
        init_load(ck_sb[:], ck.rearrange("s k f -> k s f"), ["ck_sb"])
        init_load(cv_sb[:], cv.rearrange("s k f -> k s f"), ["cv_sb"])
        for half in (0, 32):
            for g in range(4):
                init_load(wsTs_f[half:half + 16, g, :], wsd[g, 0:16, 0:16].rearrange("i j -> j i"),
                          ["wsTs_f%d" % half] if g == 0 else [("wsTs_fx", half, g)], slow=True)
        for ev in init_evs:
            ev.val = dinit.count

        def x_load(t):
            xb = t % 2
            POOL.emit(lambda e: e.dma_start(out=xbuf[:, xb, :, :],
                                            in_=xp[t * TT:(t + 1) * TT, :].rearrange("(tt p) d -> p tt d", p=128)),
                      writes=[("x", xb, tt) for tt in range(4)], dma=ds("xl%d" % xb))

        DVE.emit(lambda e: e.memset(xsb[:], 0.0), writes=[("xs", 0), ("xs1",)])
        POOL.emit(lambda e: e.dma_start(out=xsb[0:16, :], in_=xs[0:16, :]), writes=[("xs", 0)], dma=ds("xl_s0"))
        POOL.emit(lambda e: e.dma_start(out=xsb[32:48, :], in_=xs[16:32, :]), writes=[("xs1",)], dma=ds("xl_s1"))
        x_load(0)
        if NPT > 1:
            x_load(1)

        def conv_aps(i):
            kind, j = PIECES[i]
            dst = wsc[i]
            if kind in ("g1", "u1", "g2", "u2"):
                return [(dst.rearrange("p (kc n) -> p kc n", kc=8),
                         W[kind][:, j * 256:(j + 1) * 256].rearrange("(kc p) n -> p kc n", p=128))]
            if kind in ("d1", "d2"):
                return [(dst.rearrange("p (fc n) -> p fc n", fc=2),
                         W[kind][j * 256:(j + 1) * 256, :].rearrange("(fc p) n -> p fc n", p=128))]
            if kind in ("ba", "bg"):
                return [(dst.rearrange("p (kc n) -> p kc n", kc=8),
                         W[kind][:, j * 256:(j + 1) * 256].rearrange("(kc p) n -> p kc n", p=128))]
            if kind in ("o", "gv"):
                src = W["o"] if kind == "o" else W["win"]
                c0 = (j // 2) * 512 + (0 if kind == "o" else 2304)
                k0 = (j % 2) * 4
                return [(dst.rearrange("p (kc n) -> p kc n", kc=4),
                         src[:, c0:c0 + 512].rearrange("(kc p) n -> p kc n", p=128)[:, k0:k0 + 4, :])]
            win = W["win"]
            if kind == "q":
                res = []
                d3 = dst.rearrange("p (kc n) -> p kc n", kc=8)
                for lc in range(2):
                    c = 2 * j + lc
                    for g in range(2):
                        col = (g * 8 + c) * 64
                        res.append((d3[:, :, lc * 128 + g * 64:lc * 128 + (g + 1) * 64],
                                    win[:, col:col + 64].rearrange("(kc p) n -> p kc n", p=128)))
                return res
            base = {"kv": 1024, "u": 1280, "gv": 2304, "ga": 3328, "gb": 4352}[kind]
            c0 = base + j * 256
            return [(dst.rearrange("p (kc n) -> p kc n", kc=8),
                     win[:, c0:c0 + 256].rearrange("(kc p) n -> p kc n", p=128))]

        nconv = 0
        wsc_events = {}
        for i in range(NPIECE):
            for (d_ap, s_ap) in conv_aps(i):
                k = nconv % NCONV
                nconv += 1
                ev = POOL.emit(lambda e, d_ap=d_ap, s_ap=s_ap: e.dma_start(out=d_ap, in_=s_ap),
                               writes=[("convsem", k)], dma=ds("conv%d" % k))
                wsc_events.setdefault(i, []).append(ev)

        state = {"n": 0, "cur": None}
        PHASE_BASE = {"F1": 0, "M": 33, "F2": 66}

        def begin_phase(ph):
            state["cur"] = PHASE_BASE[ph]
            state["end"] = PHASE_BASE[ph] + 33

        def next_piece(expect):
            i = state["cur"]
            state["cur"] += 1
            assert i < state["end"]
            kind = PIECES[i][0]
            assert kind.rstrip("12") == expect.rstrip("12"), (PIECES[i], expect)
            slot = state["n"] % NSLOT
            state["n"] += 1
            assert slot not in state.get("pinned", ()), "ring slot still pinned"
            for ev in wsc_events[i]:
                SP.wait_ev(ev)
            SP.emit(lambda e: e.dma_start(out=ring[:, slot, :], in_=wsc[i]),
                    writes=[("ring", slot)], dma=ds("ring%d" % slot))
            return slot

        DVE.emit(lambda e: e.tensor_copy(out=identb[:], in_=identf[:]), reads=["identf"], writes=["identb"])
        ACT.emit(lambda e: e.activation(out=esink[:], in_=esink[:], func=AF.Exp), reads=[], writes=["esink"])
        def late_init():
            def head():
                PE.emit(lambda e: e.matmul(out=ps[0:16, 0:384], lhsT=tbl_sb[:], rhs=oh_sb[:], start=True, stop=True),
                        reads=["tbl_sb", "oh_sb"] + XR, writes=pcell(0))
                DVE.emit(lambda e: e.tensor_copy(out=tst[:], in_=ps[0:16, 0:384]), reads=pcell(0) + XR, writes=["tst"])
                SP.emit(lambda e: e.dma_start(out=tsc, in_=tst[:]), reads=["tst"] + XR, writes=["tsc"], dma=ds("tsc"))
            brt = bufE[:].rearrange("p a b -> p (a b)").bitcast(F32)
            cells = [("bufE", c_) for c_ in range(8)]

            def br_dma(kind):
                offp = 128 * (1 - kind)
                src = bass.AP(tensor=tsc.tensor, offset=offp, ap=[[1, 128], [384, 16], [1, 128]])
                SP.emit(lambda e: e.dma_start(out=brt[:, 0:2048].rearrange("p (h q) -> p h q", h=16), in_=src),
                        reads=["tsc"], writes=cells, dma=ds("bias%d" % kind))

            def br_mm(kind):
                for cc in range(4):
                    PE.emit(lambda e: e.matmul(out=bank(cc), lhsT=jsb[:, 0:128],
                                               rhs=brt[:, cc * 512:(cc + 1) * 512], start=True, stop=True),
                            reads=cells + ["jsb"] + XR, writes=pcell(cc))
                    dH = biasHL[:, 0, kind, :, :].rearrange("p h q -> p (h q)")[:, cc * 512:(cc + 1) * 512]
                    dL = biasHL[:, 1, kind, :, :].rearrange("p h q -> p (h q)")[:, cc * 512:(cc + 1) * 512]
                    DVE.emit(lambda e: e.tensor_copy(out=dH, in_=bank(cc)), reads=pcell(cc), writes=[("bias", kind)])
                    DVE.emit(lambda e: e.tensor_tensor(out=dL, in0=bank(cc), in1=dH, op=ALU.subtract),
                             reads=pcell(cc) + [("bias", kind)], writes=[("bias", kind)])
            def rest():
                PE.emit(lambda e: e.matmul(out=bank(4)[0:64, 0:256].rearrange("p (h q) -> p h q", h=16), lhsT=jsb[:, 128:192],
                                           rhs=brt[:, 0:2048].rearrange("p (h q) -> p h q", h=16)[:, :, 0:16], start=True, stop=True),
                        reads=[("bufE", c_) for c_ in range(8)] + ["jsb"] + XR, writes=pcell(4))
                DVE.emit(lambda e: e.tensor_copy(out=biasSHL[:, 0, :, :].rearrange("p h q -> p (h q)"), in_=bank(4)[0:64, 0:256]),
                         reads=pcell(4), writes=[("biasS", 0), ("biasS", 32)])
                DVE.emit(lambda e: e.tensor_tensor(out=biasSHL[:, 1, :, :].rearrange("p h q -> p (h q)"), in0=bank(4)[0:64, 0:256],
                                                   in1=biasSHL[:, 0, :, :].rearrange("p h q -> p (h q)"), op=ALU.subtract),
                         reads=pcell(4) + [("biasS", 0)], writes=[("biasS", 0), ("biasS", 32)])
                DVE.emit(lambda e: e.memset(biasHL[0:64, 0, 0, :, 64:128], NEG), writes=[("bias", 0)])
                DVE.emit(lambda e: e.memset(biasHL[0:64, 1, 0, :, 64:128], 0.0), writes=[("bias", 0)])
                DVE.emit(lambda e: e.memset(biasHL[64:128, 0, 1, :, 0:64], NEG), writes=[("bias", 1)])
                DVE.emit(lambda e: e.memset(biasHL[64:128, 1, 1, :, 0:64], 0.0), writes=[("bias", 1)])
                DVE.emit(lambda e: e.memset(ws_sb[0:64, :, 64:128], 0.0), reads=XR, writes=["ws_sb"])
                for g in range(4):
                    PE.emit(lambda e, g=g: e.transpose(out=ps[:, 512 + g * 128:512 + (g + 1) * 128], in_=ws_sb[:, g, :],
                                                       identity=identf[:]),
                            reads=["ws_sb", "identf"] + XR, writes=[("ps", 1)])
                DVE.emit(lambda e: e.tensor_copy(out=wsT[:].rearrange("p g i -> p (g i)"), in_=bank(1)),
                         reads=pcell(1), writes=["wsT"])
                DVE.emit(lambda e: e.tensor_copy(out=wsTs[0:16, :, :], in_=wsTs_f[0:16, :, :]), reads=["wsTs_f0"], writes=["wsTs0"])
                DVE.emit(lambda e: e.tensor_copy(out=wsTs[32:48, :, :], in_=wsTs_f[32:48, :, :]), reads=["wsTs_f32"], writes=["wsTs32"])
                for s in range(2):
                    PE.emit(lambda e, s=s: e.transpose(out=ps[:, 1024 + s * 128:1024 + (s + 1) * 128], in_=ck_sb[:, s, :],
                                                       identity=identf[:]),
                            reads=["ck_sb", "identf"] + XR, writes=[("ps", 2)])
                DVE.emit(lambda e: e.memset(kTc[:], 0.0), writes=["kTc"])
                DVE.emit(lambda e: e.memset(kT[:], 0.0), writes=[("kT", b_) for b_ in range(5)])
                DVE.emit(lambda e: e.memset(kTs[:], 0.0), writes=["kTs"])
                for k_ in range(2):
                    DVE.emit(lambda e: e.tensor_copy(out=kTc[k_ * 64:(k_ + 1) * 64, :, k_, :],
                                                     in_=ps[k_ * 64:(k_ + 1) * 64, 1024:1280].rearrange("p (s k) -> p s k", s=2)),
                             reads=[("ps", 2)], writes=["kTc"])
                DVE.emit(lambda e: e.memset(Vc[:, :, :, 64:65], 1.0), writes=["Vc1"])
                DVE.emit(lambda e: e.tensor_copy(out=Vc[:, :, :, 0:64],
                                                 in_=cv_sb[:].rearrange("p s (k d) -> p s k d", k=2)),
                         reads=["cv_sb"] + XR, writes=["Vc"])
                DVE.emit(lambda e: e.memset(Vaug[:, :, :, 64:65], 1.0), writes=["Vaug1"])
                DVE.emit(lambda e: e.memset(Vnew[:, :, 64:65], 1.0), writes=["Vnew1"])
                DVE.emit(lambda e: e.memset(bufE[:], 0.0), writes=[("bufE", c_) for c_ in range(8)])

            return [lambda: (head(), br_dma(0)), lambda: (br_mm(0), br_dma(1)), lambda: (br_mm(1), rest())]

        def newton(P, n, scale):
            v, ti, yi, t = nwv[:P, 0:n], nwi[:P, 0:n], nwy[:P, 0:n], nwt[:P, 0:n]
            y = nwy[:P, 0:n].bitcast(F32)
            DVE.emit(lambda e: e.tensor_scalar(out=v, in0=ssq[:P, 0:n], scalar1=scale, scalar2=EPS,
                                               op0=ALU.mult, op1=ALU.add), reads=["ssq"], writes=["nwv"])
            DVE.emit(lambda e: e.tensor_single_scalar(out=ti, in_=nwv[:P, 0:n].bitcast(I32), scalar=1,
                                                      op=ALU.arith_shift_right), reads=["nwv"], writes=["nwi"])
            DVE.emit(lambda e: e.tensor_scalar(out=yi, in0=ti, scalar1=-1.0, scalar2=MAGIC,
                                               op0=ALU.mult, op1=ALU.add), reads=["nwi"], writes=["nwy"])
            DVE.emit(lambda e: e.tensor_scalar(out=v, in0=v, scalar1=-0.5, scalar2=None, op0=ALU.mult),
                     reads=["nwv", "nwi"], writes=["nwv"])
            for it in range(3):
                DVE.emit(lambda e: e.tensor_tensor(out=t, in0=y, in1=y, op=ALU.mult), reads=["nwy"], writes=["nwt"])
                DVE.emit(lambda e: e.tensor_tensor(out=t, in0=t, in1=v, op=ALU.mult), reads=["nwt", "nwv"], writes=["nwt"])
                if it < 2:
                    DVE.emit(lambda e: e.scalar_tensor_tensor(out=y, in0=t, scalar=1.5, in1=y, op0=ALU.add, op1=ALU.mult),
                             reads=["nwy", "nwt"], writes=["nwy"])
                else:
                    DVE.emit(lambda e: e.scalar_tensor_tensor(out=rstd[:P, 0:n], in0=t, scalar=1.5, in1=y, op0=ALU.add, op1=ALU.mult),
                             reads=["nwy", "nwt"], writes=["rstd"])

        class Tile:
            pass

        def prenorm_A1(tl):
            P = tl.P
            for tt in range(tl.ntt):
                ACT.emit(lambda e: e.activation(out=junk[:P, :], in_=tl.xap(tt), func=AF.Square,
                                                accum_out=ssq[:P, tt:tt + 1]),
                         reads=tl.xcells(tt), writes=["tmpB", "ssq"])
            newton(P, tl.ntt, 1.0 / D)

        def prenorm_A2(tl):
            P = tl.P
            for tt in range(tl.ntt):
                ACT.emit(lambda e: e.activation(out=hbuf[:P, tt, :], in_=tl.xap(tt), func=AF.Copy,
                                                scale=rstd[:P, tt:tt + 1]),
                         reads=tl.xcells(tt) + ["rstd"], writes=[("hbuf", tt)])

        def prenorm_B(tl, gi, hi):
            P = tl.P
            hT = hTs[hi]
            for tt in range(tl.ntt):
                b = 4 + tt

                def trs(e):
                    ins = None
                    o = bankbf(b).rearrange("p (c t) -> p c t", c=8)
                    for kc in range(8):
                        ins = e.transpose(out=o[:, kc, 0:P], in_=hbuf[:P, tt, kc * 128:(kc + 1) * 128],
                                          identity=identb[:P, :P])
                    return ins
                PE.emit(trs, reads=[("hbuf", tt), "identb"], writes=pcell(b))
                DVE.emit(lambda e: e.tensor_tensor(
                    out=hT[:, :, tt * 128:tt * 128 + P],
                    in0=bankbf(b).rearrange("p (c t) -> p c t", c=8)[:, :, 0:P],
                    in1=gcol[:, gi // 2, :].unsqueeze(2).to_broadcast([128, 8, P]), op=ALU.mult),
                    reads=pcell(b) + ["gcol", ("gcolx", 1), ("gcolx", 2)], writes=[("hT", hi, tt)])

        def post_norm(tl, gi, factor, pool_x=None):
            if pool_x is None:
                pool_x = tl.use_pool
            P = tl.P
            stmp2 = stmp[:, 0:2, :].rearrange("p a b -> p (a b)")
            T = [(tmpA, ["tmpA"]), (stmp2, [("stmp", 0), ("stmp", 1)])]
            for tt in range(tl.ntt):
                ACT.emit(lambda e: e.activation(out=junk[:P, :], in_=ps[:P, tt * 1024:(tt + 1) * 1024],
                                                func=AF.Square, accum_out=ssq[:P, tt:tt + 1]),
                         reads=pcell(2 * tt) + pcell(2 * tt + 1), writes=["tmpB", "ssq", ("sqdone", tt)])

            def ymul(tt):
                tb, tc = T[tt % 2]
                DVE.emit(lambda e: e.tensor_tensor(out=tb[:P, :], in0=ps[:P, tt * 1024:(tt + 1) * 1024],
                                                   in1=gb[:P, gi // 2, :], op=ALU.mult),
                         reads=pcell(2 * tt) + pcell(2 * tt + 1) + ["gb", ("sqdone", tt)], writes=tc)

            def xupd(tt):
                tb, tc = T[tt % 2]
                if pool_x and tt >= 2:
                    POOL.emit(lambda e: e.tensor_scalar(out=tb[:P, :], in0=tb[:P, :], scalar1=rstd[:P, tt:tt + 1], scalar2=0.0,
                                                        op0=ALU.mult, op1=ALU.add), reads=tc + ["rstd"], writes=tc)
                    POOL.emit(lambda e: e.tensor_tensor(out=tl.xap(tt), in0=tb[:P, :], in1=tl.xap(tt), op=ALU.add),
                              reads=tc + tl.xcells(tt), writes=[tl.xcells(tt)[0]])
                    return
                DVE.emit(lambda e: e.scalar_tensor_tensor(
                    out=tl.xap(tt), in0=tb[:P, :], scalar=rstd[:P, tt:tt + 1], in1=tl.xap(tt),
                    op0=ALU.mult, op1=ALU.add),
                    reads=tc + ["rstd"] + tl.xcells(tt), writes=[tl.xcells(tt)[0]])

            for tt in range(min(2, tl.ntt)):
                ymul(tt)
            newton(P, tl.ntt, 1.0 / D)
            if factor != 1.0:
                DVE.emit(lambda e: e.tensor_scalar(out=rstd[:P, 0:tl.ntt], in0=rstd[:P, 0:tl.ntt], scalar1=factor,
                                                   scalar2=None, op0=ALU.mult), reads=["rstd"], writes=["rstd"])
            for tt in range(tl.ntt):
                if tt >= 2:
                    ymul(tt)
                xupd(tt)

        def fm_group(slot, lc, rhs_buf, b, NC, reads):
            def fn(e):
                ins = None
                for kc in range(8):
                    ins = e.matmul(out=bank(b)[:, 0:NC], lhsT=ring[:, slot, kc * 256 + lc * 128:kc * 256 + (lc + 1) * 128],
                                   rhs=rhs_buf[:, kc, 0:NC], start=(kc == 0), stop=(kc == 7))
                return ins
            PE.emit(fn, reads=[("ring", slot)] + reads, writes=pcell(b))

        def tm_group(slot, lhs_buf, tt, P, b, col0, reads, wcells):
            def fn(e):
                ins = None
                for kc in range(8):
                    ins = e.matmul(out=bank(b)[:P, col0:col0 + 256], lhsT=lhs_buf[:, kc, tt * 128:tt * 128 + P],
                                   rhs=ring[:, slot, kc * 256:(kc + 1) * 256], start=(kc == 0), stop=(kc == 7))
                return ins
            PE.emit(fn, reads=[("ring", slot)] + reads, writes=wcells)

        hT_cells = lambda tl, hi: [("hT", hi, tt) for tt in range(tl.ntt)]

        def ffn(tl, f, gi_post, hi, hooks, early=None, co=None):
            P, NC = tl.P, tl.NC
            hT = hTs[hi]
            begin_phase("F%d" % f)
            if co is not None:
                actS = PT[:].rearrange("p a b c d -> p (a b c d)")[:, 0:NFC * 64].rearrange("p (f t) -> p f t", f=NFC)
                ptc = [("PT", a_, b_) for a_ in range(2) for b_ in range(2)]
                coh = hT_cells(co, 2)
            for j in range(11):
                if early is not None and j in early:
                    early[j]()
                if j in (6, 8, 10):
                    hooks[(j - 6) // 2]()
                sg_ = next_piece("g%d" % f)
                su_ = next_piece("u%d" % f)
                for lc in range(2):
                    fc = 2 * j + lc
                    bg_, bu_ = (fc % 2) * 2, (fc % 2) * 2 + 1
                    fm_group(sg_, lc, hT, bg_, NC, hT_cells(tl, hi))
                    fm_group(su_, lc, hT, bu_, NC, hT_cells(tl, hi))
                    sp_ = fc % 2
                    ACT.emit(lambda e, bg_=bg_, sp_=sp_: e.activation(out=sgt[:, sp_, 0:NC], in_=bank(bg_)[:, 0:NC], func=AF.Silu),
                             reads=pcell(bg_), writes=[("sgt", sp_)])
                    DVE.emit(lambda e, bu_=bu_, sp_=sp_, fc=fc: e.tensor_tensor(out=act[:, fc, 0:NC], in0=bank(bu_)[:, 0:NC],
                                                                               in1=sgt[:, sp_, 0:NC], op=ALU.mult),
                             reads=pcell(bu_) + [("sgt", sp_)], writes=[("abc", fc)])
                    if co is not None:
                        fm_group(sg_, lc, hTs[2], 4, 64, coh)
                        fm_group(su_, lc, hTs[2], 5, 64, coh)
                        ACT.emit(lambda e: e.activation(out=stmp[:, 2 + sp_, 0:64], in_=bank(4)[:, 0:64], func=AF.Silu),
                                 reads=pcell(4), writes=[("stmp", 2 + sp_)])
                        DVE.emit(lambda e: e.tensor_tensor(out=actS[:, fc, :], in0=bank(5)[:, 0:64], in1=stmp[:, 2 + sp_, 0:64],
                                                           op=ALU.mult), reads=pcell(5) + [("stmp", 2 + sp_)], writes=ptc)
            for j in range(11):
                sd_ = next_piece("d%d" % f)
                for tt in range(tl.ntt):
                    for half in range(2):
                        b = tt * 2 + half

                        def fn(e, j=j, tt=tt, half=half, b=b, sd_=sd_):
                            ins = None
                            for lc in range(2):
                                ins = e.matmul(out=bank(b)[:P, :], lhsT=act[:, 2 * j + lc, tt * 128:tt * 128 + P],
                                               rhs=ring[:, sd_, lc * 1024 + half * 512:lc * 1024 + (half + 1) * 512],
                                               start=(j == 0 and lc == 0), stop=(j == 10 and lc == 1))
                            return ins
                        PE.emit(fn, reads=[("ring", sd_), ("abc", 2 * j), ("abc", 2 * j + 1)], writes=pcell(b))
            post_norm(tl, gi_post, 0.5, pool_x=(tl.use_pool and not (f == 1 and tl.t <= 1)))
            if co is not None:
                state["cur"] = PHASE_BASE["F%d" % f] + 22
                for j in range(11):
                    sd_ = next_piece("d%d" % f)
                    for half in range(2):
                        def fn2(e):
                            ins = None
                            for lc in range(2):
                                ins = e.matmul(out=bank(half)[:64, :], lhsT=actS[:, 2 * j + lc, :],
                                               rhs=ring[:, sd_, lc * 1024 + half * 512:lc * 1024 + (half + 1) * 512],
                                               start=(j == 0 and lc == 0), stop=(j == 10 and lc == 1))
                            return ins
                        PE.emit(fn2, reads=[("ring", sd_)] + ptc, writes=pcell(half))
                post_norm(co, gi_post, 0.5, pool_x=False)

        def mixer(tl, hi, hooks):
            P, NC = tl.P, tl.NC
            samp = tl.sample
            hT = hTs[hi]
            begin_phase("M")
            hc = hT_cells(tl, hi)
            EW = POOL if tl.use_pool else DVE
            vn = bufD
            skv = next_piece("kv")
            fm_group(skv, 0, hT, 0, NC, hc)
            if samp:
                for k_ in range(2):
                    ACT.emit(lambda e: e.copy(out=kTs[k_ * 64:(k_ + 1) * 64, k_, :], in_=bank(0)[k_ * 64:(k_ + 1) * 64, 0:64]),
                             reads=pcell(0), writes=["kTs"])
            else:
                for k_ in range(2):
                    ACT.emit(lambda e: e.copy(out=kT[k_ * 64:(k_ + 1) * 64, k_, 128:640], in_=bank(0)[k_ * 64:(k_ + 1) * 64, :]),
                             reads=pcell(0), writes=[("kT", b_) for b_ in range(1, 5)])
            for tt in range(tl.ntt):
                b = 1 + tt % 3
                tm_group(skv, hT, tt, P, b, 0, hc, pcell(b))
                src = bank(b)[:P, 0:256]
                if samp:
                    ACT.emit(lambda e: e.copy(out=Vnew[:, :, 0:64], in_=src[:, 128:256].rearrange("p (k d) -> p k d", k=2)),
                             reads=pcell(b), writes=["Vnew"])
                    ACT.emit(lambda e: e.copy(out=kvst[0:64, :], in_=src), reads=pcell(b), writes=["kvst"])
                    for s_ in range(2):
                        POOL.emit(lambda e: e.dma_start(out=nks[16 * s_:16 * s_ + 16, :], in_=kvst[32 * s_:32 * s_ + 16, 0:128]),
                                  reads=["kvst"], dma=ds("o_nks%d" % s_))
                        POOL.emit(lambda e: e.dma_start(out=nvs[16 * s_:16 * s_ + 16, :], in_=kvst[32 * s_:32 * s_ + 16, 128:256]),
                                  reads=["kvst"], dma=ds("o_nvs%d" % s_))
                else:
                    ACT.emit(lambda e: e.copy(out=Vaug[:, 1 + tt, :, 0:64], in_=src[:, 128:256].rearrange("p (k d) -> p k d", k=2)),
                             reads=pcell(b), writes=[("V", 1 + tt)])
                    if tl.last and tt == 3:
                        ACT.emit(lambda e: e.copy(out=kvst[:, :], in_=src), reads=pcell(b), writes=["kvst"])
                        POOL.emit(lambda e: e.dma_start(out=wkp, in_=kvst[:, 0:128]), reads=["kvst"], dma=ds("o_wkp"))
                        POOL.emit(lambda e: e.dma_start(out=wvp, in_=kvst[:, 128:256]), reads=["kvst"], dma=ds("o_wvp"))
            for j in range(4):
                sq_ = next_piece("q")
                for lc in range(2):
                    c = 2 * j + lc
                    b = c % 4
                    fm_group(sq_, lc, hT, b, NC, hc)
                    ACT.emit(lambda e: e.mul(out=qT[:, c, 0:NC], in_=bank(b)[:, 0:NC], mul=0.125),
                             reads=pcell(b), writes=[("abc", c)])

            if samp:
                units = [(s_, None) for s_ in range(2)]
            else:
                units = [(None, jq) for jq in range(4)]
            Q = 16 if samp else 128
            NQ = 4 * Q
            ginfo = {}

            def att_scores(gi_):
                ui, grp = gi_ // 4, gi_ % 4
                s_, jq = units[ui]
                k = grp // 2
                c0 = (grp % 2) * 4
                h0 = k * 8 + c0
                gpar = gi_ % 2
                if samp:
                    c_lo = 32 * s_
                    rhs_q = qT[:, c0:c0 + 4, c_lo:c_lo + 16]
                    blocks = [dict(lhsT=kTc[:, s_, k, :], M=128, p0=0, pn=128,
                                   idl=identb[:, :], bh=biasHL[:, 0, 0, h0:h0 + 4, 0:16], bl=biasHL[:, 1, 0, h0:h0 + 4, 0:16],
                                   v=Vc[:, s_, k, :], rd=["kTc", ("bias", 0)], vr=["Vc", "Vc1"]),
                              dict(lhsT=kTs[:, k, :], M=64, p0=c_lo, pn=16,
                                   idl=identb[c_lo:c_lo + 16, 0:64], bh=biasSHL[c_lo:c_lo + 16, 0, h0:h0 + 4, :],
                                   bl=biasSHL[c_lo:c_lo + 16, 1, h0:h0 + 4, :], v=Vnew[c_lo:c_lo + 16, k, :],
                                   rd=["kTs", ("biasS", c_lo)], vr=["Vnew", "Vnew1"])]
                else:
                    rhs_q = qT[:, c0:c0 + 4, jq * 128:(jq + 1) * 128]
                    blocks = []
                    for kind, blk in ((0, jq), (1, jq + 1)):
                        if kind == 0 and tl.first and jq == 0:
                            continue
                        blocks.append(dict(lhsT=kT[:, k, blk * 128:(blk + 1) * 128], M=128,
                                           p0=0, pn=128, idl=identb[:, :], bh=biasHL[:, 0, kind, h0:h0 + 4, :],
                                           bl=biasHL[:, 1, kind, h0:h0 + 4, :], v=Vaug[:, blk, k, :],
                                           rd=[("kT", blk), ("bias", kind)], vr=[("V", blk), "Vaug1"]))
                ginfo[gi_] = (blocks, gpar, h0, ui)
                for bi, bk in enumerate(blocks):
                    b = gpar * 2 + bi
                    p0, pn = bk["p0"], bk["pn"]
                    def sfn(e):
                        o = bank(b)[0:bk["M"], 0:NQ].rearrange("p (g q) -> p g q", g=4)
                        e.matmul(out=o, lhsT=bk["lhsT"], rhs=rhs_q, start=True, stop=False)
                        e.matmul(out=o, lhsT=bk["idl"], rhs=bk["bh"], start=False, stop=False)
                        return e.matmul(out=o, lhsT=bk["idl"], rhs=bk["bl"], start=False, stop=True)
                    PE.emit(sfn, reads=[("abc", c) for c in range(c0, c0 + 4)] + bk["rd"] + ["identb"], writes=pcell(b))
                    ACT.emit(lambda e: e.activation(
                        out=PT[p0:p0 + pn, gpar, bi, :, 0:Q],
                        in_=bank(b)[p0:p0 + pn, 0:NQ].rearrange("p (g q) -> p g q", g=4),
                        func=AF.Exp), reads=pcell(b), writes=[("PT", gpar, bi)])

            def att_pv(gi_):
                blocks, gpar, h0, ui = ginfo[gi_]
                grp = gi_ % 4
                ob = ui % 2
                nb = len(blocks)
                bpv = 4

                def pvfn(e):
                    ins = None
                    for g in range(4):
                        for bi, bk in enumerate(blocks):
                            p0, pn = bk["p0"], bk["pn"]
                            ins = e.matmul(out=bank(bpv)[0:Q, g * 65:(g + 1) * 65], lhsT=PT[p0:p0 + pn, gpar, bi, g, 0:Q],
                                           rhs=bk["v"], start=(bi == 0), stop=(bi == nb - 1))
                    return ins
                rds = [("PT", gpar, bi) for bi in range(nb)]
                for bk in blocks:
                    rds += bk["vr"]
                PE.emit(pvfn, reads=rds, writes=pcell(bpv))
                pv3 = bank(bpv)[0:Q, 0:260].rearrange("p (g d) -> p g d", g=4)
                DVE.emit(lambda e: e.tensor_tensor(out=den[0:Q, :], in0=pv3[:, :, 64], in1=esink[0:Q, h0:h0 + 4], op=ALU.add),
                         reads=pcell(bpv) + ["esink"], writes=["den"])
                DVE.emit(lambda e: e.reciprocal(out=rden[0:Q, :], in_=den[0:Q, :]), reads=["den"], writes=["rden"])
                DVE.emit(lambda e: e.tensor_tensor(out=obuf[0:Q, ob, h0:h0 + 4, :], in0=pv3[:, :, 0:64],
                                                   in1=rden[0:Q, :].unsqueeze(2).to_broadcast([Q, 4, 64]), op=ALU.mult),
                         reads=pcell(bpv) + ["rden"], writes=[("obuf", ob, grp)])
            def att_tr(ui):
                if True:
                    ob = ui % 2
                    bt = 5
                    s_, jq = units[ui]

                    def trs(e):
                        ins = None
                        o = bankbf(bt).rearrange("p (c t) -> p c t", c=8)
                        for c in range(8):
                            ins = e.transpose(out=o[:, c, 0:Q], in_=obuf[0:Q, ob, 2 * c:2 * c + 2, :].rearrange("p h d -> p (h d)"),
                                              identity=identb[0:Q, 0:Q])
                        return ins
                    PE.emit(trs, reads=[("obuf", ob, g_) for g_ in range(4)] + ["identb"], writes=pcell(bt))
                    cols = (32 * s_, 32 * s_ + 16) if samp else (jq * 128, (jq + 1) * 128)
                    ACT.emit(lambda e: e.copy(out=AT[:, :, cols[0]:cols[1]],
                                              in_=bankbf(bt).rearrange("p (c t) -> p c t", c=8)[:, :, 0:Q]),
                             reads=pcell(bt), writes=[("abc", 8 + c_) for c_ in range(8)])

            fillers = []
            gvslots = []

            hbuf32 = hbuf[:].rearrange("p a b -> p (a b)").bitcast(F32)

            def tbsel(tt):
                if tt == 0:
                    return tmpB, ["tmpB"]
                if tt == 1:
                    return sgt[:].rearrange("p a b -> p (a b)"), [("sgt", 0), ("sgt", 1)]
                return hbuf32[:, (tt - 2) * 1024:(tt - 1) * 1024], [("hbuf", 2 * (tt - 2)), ("hbuf", 2 * (tt - 2) + 1)]

            def gv_unit(tt):
                gv_unit_a(tt)
                gv_unit_b(tt)

            def gv_unit_a(tt):
                if not gvslots:
                    for j in range(4):
                        gvslots.append(next_piece("gv"))
                tb, tbc = tbsel(tt)
                b0 = 2 * tt
                for half in range(2):
                    def gvfn(e):
                        ins = None
                        for j2 in range(2):
                            for kcl in range(4):
                                kc = j2 * 4 + kcl
                                ins = e.matmul(out=bank(b0 + half)[:P, :], lhsT=hT[:, kc, tt * 128:tt * 128 + P],
                                               rhs=ring[:, gvslots[2 * half + j2], kcl * 512:(kcl + 1) * 512],
                                               start=(kc == 0), stop=(kc == 7))
                        return ins
                    PE.emit(gvfn, reads=[("ring", gvslots[2 * half]), ("ring", gvslots[2 * half + 1])] + hc, writes=pcell(b0 + half))
                ACT.emit(lambda e: e.activation(out=tb[:P, :], in_=ps[:P, b0 * 512:(b0 + 2) * 512], func=AF.Gelu),
                         reads=pcell(b0) + pcell(b0 + 1), writes=tbc)
                for hh in range(2):
                    DVE.emit(lambda e: e.bn_stats(out=bnst[:P, hh, :], in_=tb[:P, hh * 512:(hh + 1) * 512]),
                             reads=tbc, writes=[("bnst", hh)])
                DVE.emit(lambda e: e.bn_aggr(out=mv4[:P, tt, :], in_=bnst[:P, :, :].rearrange("p a b -> p (a b)")),
                         reads=[("bnst", 0), ("bnst", 1)], writes=[("mv4", tt)])
                DVE.emit(lambda e: e.tensor_copy(out=ssq[:P, tt:tt + 1], in_=mv4[:P, tt, 1:2]), reads=[("mv4", tt)], writes=["ssq"])

            def ln_rstd_all():
                n = tl.ntt
                newton(P, n, 1.0)
                DVE.emit(lambda e: e.tensor_copy(out=lnp[:P, 0:n, 0], in_=rstd[:P, 0:n]), reads=["rstd"],
                         writes=[("lnp", t_) for t_ in range(n)])
                DVE.emit(lambda e: e.tensor_tensor(out=lnp[:P, 0:n, 1], in0=mv4[:P, 0:n, 0], in1=rstd[:P, 0:n], op=ALU.mult),
                         reads=["rstd"] + [("mv4", t_) for t_ in range(n)], writes=[("lnp", t_) for t_ in range(n)])
                DVE.emit(lambda e: e.tensor_scalar(out=lnp[:P, 0:n, 1], in0=lnp[:P, 0:n, 1], scalar1=-1.0, scalar2=None, op0=ALU.mult),
                         reads=[("lnp", t_) for t_ in range(n)], writes=[("lnp", t_) for t_ in range(n)])

            def gv_unit_b(tt):
                tb, tbc = tbsel(tt)
                ACT.emit(lambda e: e.activation(out=tmpA[:P, :], in_=tb[:P, :], func=AF.Identity,
                                                scale=lnp[:P, tt, 0:1], bias=lnp[:P, tt, 1:2]),
                         reads=tbc + [("lnp", tt)], writes=["tmpA"])
                EW.emit(lambda e: e.tensor_tensor(out=tmpA[:P, :], in0=tmpA[:P, :], in1=lngb[:P, :], op=ALU.mult),
                        reads=["tmpA", "lngb"], writes=["tmpA"])
                if samp:
                    EW.emit(lambda e: e.tensor_tensor(out=tmpB[0:64, :], in0=tmpA[:P, :], in1=lnbb[:P, :], op=ALU.add),
                            reads=["tmpA", "lnbb"], writes=["tmpB"])
                    EW.emit(lambda e: e.tensor_copy(out=vn[:P, 0, :], in_=tmpB[0:64, :]), reads=["tmpB"], writes=[("bufD", 0)])
                    for s_ in range(2):
                        SP.emit(lambda e: e.dma_start(out=gvs[16 * s_:16 * s_ + 16, :], in_=tmpB[32 * s_:32 * s_ + 16, :]),
                                reads=["tmpB"], dma=ds("o_gvs%d" % s_))
                else:
                    EW.emit(lambda e: e.tensor_tensor(out=vn[:P, tt, :], in0=tmpA[:P, :], in1=lnbb[:P, :], op=ALU.add),
                            reads=["tmpA", "lnbb"], writes=[("bufD", tt)])

            ustate = {}
            bufE32 = bufE[:].rearrange("p a b -> p (a b)").bitcast(F32)

            def ustage(c):
                if c < 4:
                    return bufE32[:, c * 512:(c + 1) * 512], [("bufE", 2 * c), ("bufE", 2 * c + 1)]
                return stmp[:, c - 4, :], [("stmp", c - 4)]

            def u_chunk(c):
                if c % 2 == 0:
                    ustate["slot"] = next_piece("u")
                b = 6 + c % 2
                fm_group(ustate["slot"], c % 2, hT, b, NC, hc)
                st, stc = ustage(c)
                DVE.emit(lambda e: e.tensor_copy(out=st[:, 0:NC], in_=bank(b)[:, 0:NC]), reads=pcell(b), writes=stc)

            def u_gelu_batch():
                for c in range(8):
                    st, stc = ustage(c)
                    ACT.emit(lambda e: e.activation(out=uT[:, c, 0:NC], in_=st[:, 0:NC], func=AF.Gelu),
                             reads=stc, writes=[("abc", 16 + c)])

            for tt in range(tl.ntt):
                gv_unit_a(tt)
            ln_rstd_all()
            uq = list(range(8))
            for tt in range(tl.ntt):
                for _ in range(2):
                    fillers.append(lambda c=uq.pop(0): u_chunk(c))
                fillers.append(lambda tt=tt: gv_unit_b(tt))
            while uq:
                fillers.append(lambda c=uq.pop(0): u_chunk(c))
            ng = 4 * len(units)
            steps = []
            for i in range(ng):
                steps.append(("S", i))
                if i >= 1:
                    steps.append(("P", i - 1))
                    if (i - 1) % 4 == 3:
                        steps.append(("X", None))
                        steps.append(("T", (i - 1) // 4))
            steps.append(("P", ng - 1))
            steps.append(("T", (ng - 1) // 4))
            for kind, i in steps:
                if kind == "S":
                    att_scores(i)
                elif kind == "P":
                    att_pv(i)
                elif kind == "T":
                    att_tr(i)
                if fillers:
                    fillers.pop(0)()
            while fillers:
                fillers.pop(0)()
            u_gelu_batch()
            if samp:
                DVE.emit(lambda e: e.memset(bufE[:], 0.0), writes=[("bufE", c_) for c_ in range(8)])
            if not samp and not tl.last:
                DVE.emit(lambda e: e.tensor_copy(out=kT[:, :, 0:128], in_=kT[:, :, 512:640]), reads=[("kT", 4)], writes=[("kT", 0)])
                DVE.emit(lambda e: e.tensor_copy(out=Vaug[:, 0, :, 0:64], in_=Vaug[:, 4, :, 0:64]), reads=[("V", 4)], writes=[("V", 0)])
            AT_cells = [("abc", 8 + c_) for c_ in range(8)]

            def gating(c):
                g = c // 2
                b = c % 2
                if samp:
                    for s_ in range(2):
                        c_lo = 32 * s_
                        PE.emit(lambda e: e.matmul(out=bank(b)[:, c_lo:c_lo + 16], lhsT=vn[c_lo:c_lo + 16, 0, c * 128:(c + 1) * 128],
                                                   rhs=wsTs[c_lo:c_lo + 16, g, :], start=True, stop=True),
                                reads=[("bufD", 0), "wsTs%d" % c_lo], writes=pcell(b))
                        DVE.emit(lambda e: e.tensor_tensor(out=stmp[:, b, c_lo:c_lo + 16], in0=bank(b)[:, c_lo:c_lo + 16],
                                                           in1=bsb[:, g, 0:16], op=ALU.add),
                                 reads=pcell(b) + ["bsb"], writes=[("stmp", b)])
                        EW.emit(lambda e: e.tensor_tensor(out=gmT[:, c, c_lo:c_lo + 16], in0=stmp[:, b, c_lo:c_lo + 16],
                                                          in1=uT[:, c, c_lo:c_lo + 16], op=ALU.mult),
                                reads=[("stmp", b), ("abc", 16 + c)], writes=[("bufE", c)])
                else:
                    def gfn(e):
                        ins = None
                        for tt in range(4):
                            ins = e.matmul(out=bank(b)[:, tt * 128:(tt + 1) * 128], lhsT=vn[:, tt, c * 128:(c + 1) * 128],
                                           rhs=wsT[:, g, :], start=True, stop=True)
                        return ins
                    PE.emit(gfn, reads=[("bufD", tt) for tt in range(4)] + ["wsT"], writes=pcell(b))
                    DVE.emit(lambda e: e.tensor_tensor(
                        out=stmp[:, b, :].rearrange("p (t i) -> p t i", t=4), in0=bank(b).rearrange("p (t i) -> p t i", t=4),
                        in1=bsb[:, g, :].unsqueeze(1).to_broadcast([128, 4, 128]), op=ALU.add),
                        reads=pcell(b) + ["bsb"], writes=[("stmp", b)])
                    EW.emit(lambda e: e.tensor_tensor(out=gmT[:, c, :], in0=stmp[:, b, :], in1=uT[:, c, :], op=ALU.mult),
                            reads=[("stmp", b), ("abc", 16 + c)], writes=[("bufE", c)])

            for j in range(4):
                if j == 0:
                    hooks[0]()
                if j == 2:
                    hooks[1]()
                sga_ = next_piece("ga")
                sgb_ = next_piece("gb")
                for lc in range(2):
                    c = 2 * j + lc
                    fm_group(sga_, lc, hT, 2 + c % 2, NC, hc)
                    ACT.emit(lambda e: e.activation(out=sgaT[:, c, 0:NC], in_=bank(2 + c % 2)[:, 0:NC], func=AF.Sigmoid),
                             reads=pcell(2 + c % 2), writes=[("abc", c)])
                for lc in range(2):
                    c = 2 * j + lc
                    gating(c)
                for lc in range(2):
                    c = 2 * j + lc
                    fm_group(sgb_, lc, hT, 4 + c % 2, NC, hc)
                    ACT.emit(lambda e: e.activation(out=sgbT[:, c, 0:NC], in_=bank(4 + c % 2)[:, 0:NC], func=AF.Sigmoid),
                             reads=pcell(4 + c % 2), writes=[("abc", 16 + c)])
            mT = bufD.rearrange("p a b -> p (a b)").rearrange("p (c t) -> p c t", c=8)
            gm_cells = [("bufE", c) for c in range(8)]
            for j in range(4):
                sa_ = next_piece("ba")
                sb_ = next_piece("bg")
                for lc in range(2):
                    oc = 2 * j + lc
                    bA = (oc % 2) * 2
                    bB = bA + 1
                    fm_group(sa_, lc, AT, bA, NC, AT_cells)
                    fm_group(sb_, lc, gmT, bB, NC, gm_cells)
                    if oc % 2 == 0:
                        t1, t2, c1, c2 = stmp[:, 2, 0:NC], stmp[:, 3, 0:NC], ("stmp", 2), ("stmp", 3)
                    else:
                        t1, t2, c1, c2 = sgt[:, 0, 0:NC], sgt[:, 1, 0:NC], ("sgt", 0), ("sgt", 1)
                    DVE.emit(lambda e: e.tensor_tensor(out=t1, in0=bank(bA)[:, 0:NC], in1=sgaT[:, oc, 0:NC], op=ALU.mult),
                             reads=pcell(bA) + [("abc", oc)], writes=[c1])
                    DVE.emit(lambda e: e.tensor_tensor(out=t2, in0=bank(bB)[:, 0:NC], in1=sgbT[:, oc, 0:NC], op=ALU.mult),
                             reads=pcell(bB) + [("abc", 16 + oc)], writes=[c2])
                    EW.emit(lambda e: e.tensor_tensor(out=mT[:, oc, 0:NC], in0=t1, in1=t2, op=ALU.add),
                            reads=[c1, c2], writes=[("bufD", t_) for t_ in range(4)])
            hooks[2]()
            for j in range(4):
                s_o = next_piece("o")
                half, j2 = j // 2, j % 2
                for tt in range(tl.ntt):
                    b = tt * 2 + half

                    def ofn(e):
                        ins = None
                        for kcl in range(4):
                            kc = j2 * 4 + kcl
                            ins = e.matmul(out=bank(b)[:P, :], lhsT=mT[:, kc, tt * 128:tt * 128 + P],
                                           rhs=ring[:, s_o, kcl * 512:(kcl + 1) * 512], start=(kc == 0), stop=(kc == 7))
                        return ins
                    PE.emit(ofn, reads=[("ring", s_o)] + [("bufD", t_) for t_ in range(4)], writes=pcell(b))
            post_norm(tl, 3, 1.0)

        class Tile:
            pass

        def mk_tile(t):
            tl = Tile()
            tl.sample = (t < 0)
            tl.first = (t == 0)
            tl.last = (t == NPT - 1)
            tl.t = t
            tl.use_pool = (t >= 1)
            if tl.sample:
                tl.P, tl.NC, tl.ntt = 64, 64, 1
                tl.xap = lambda tt: xsb[:, :]
                tl.xcells = lambda tt: [("xs", 0), ("xs1",)]
            else:
                xb = t % 2
                tl.P, tl.NC, tl.ntt = 128, TT, 4
                tl.xap = lambda tt: xbuf[:, xb, tt, :]
                tl.xcells = lambda tt: [("x", xb, tt)]
            return tl

        S = mk_tile(-1)
        PT_ = [mk_tile(t) for t in range(NPT)]
        sched = [("F1", PT_[0])]
        if NPT > 1:
            sched.append(("F1", PT_[1]))
        sched += [("M", PT_[0]), ("M", S), ("F2", PT_[0])] if NPT > 1 else [("M", S), ("M", PT_[0]), ("F2", S), ("F2", PT_[0])]
        for n in range(1, NPT):
            sched.append(("M", PT_[n]))
            if n + 1 < NPT:
                sched.append(("F1", PT_[n + 1]))
            else:
                sched.append(("F2", S))
            sched.append(("F2", PT_[n]))
        PRE_GI = {"F1": 0, "M": 2, "F2": 4}

        def finish_tile(tl):
            if tl.sample:
                for s_ in range(2):
                    POOL.emit(lambda e: e.dma_start(out=ys[16 * s_:16 * s_ + 16, :], in_=xsb[32 * s_:32 * s_ + 16, :]),
                              reads=tl.xcells(0), dma=ds("o_ys%d" % s_))
            else:
                t, xb = tl.t, tl.t % 2
                for tt in range(4):
                    POOL.emit(lambda e: e.dma_start(out=yp[t * TT + tt * 128:t * TT + (tt + 1) * 128, :],
                                                    in_=xbuf[:, xb, tt, :]),
                              reads=[("x", xb, tt)], dma=ds("ys%d_%d" % (xb, tt)))
                if t + 2 < NPT:
                    x_load(t + 2)

        ph0, tl0 = sched[0]
        prenorm_A1(S)
        prenorm_A2(S)
        prenorm_B(S, PRE_GI["F1"], 2)
        prenorm_A1(tl0)
        prenorm_A2(tl0)
        prenorm_B(tl0, PRE_GI[ph0], 0)
        li = late_init()
        EARLY = {0: {1: li[0], 8: li[1]}, 1: {2: li[2]}}
        for k, (ph, tl) in enumerate(sched):
            hi = k % 2
            serial_next = False
            if k + 1 < len(sched):
                nph, ntl = sched[k + 1]
                if ntl is tl:
                    serial_next = True
                    hooks = [lambda: None] * 3
                else:
                    hooks = [lambda: prenorm_A1(ntl), lambda: prenorm_A2(ntl), lambda: prenorm_B(ntl, PRE_GI[nph], 1 - hi)]
            else:
                hooks = [lambda: None] * 3
            if ph == "F1":
                ffn(tl, 1, 1, hi, hooks, early=EARLY.get(k), co=(S if k == 0 else None))
            elif ph == "F2":
                ffn(tl, 2, 5, hi, hooks)
                finish_tile(tl)
            else:
                mixer(tl, hi, hooks)
            if serial_next:
                prenorm_A1(ntl)
                prenorm_A2(ntl)
                prenorm_B(ntl, PRE_GI[nph], 1 - hi)
        for key, d in dsem.items():
            if key.startswith("o_") or key.startswith("ys"):
                POOL.wait_ev(Ev(key, d.count))

        sem_keys = ["PE", "ACT", "DVE", "POOL"] + list(dsem.keys())
        sems = {}
        for kname in sem_keys:
            sems[kname] = es.enter_context(nc.semaphore("s_" + kname))
        block = es.enter_context(nc.Block())

        class FirstWait:
            def __init__(self, e):
                self._e = e
                self._pending = None

            def __getattr__(self, name):
                real = getattr(self._e, name)
                if not callable(real):
                    return real

                def call(*a, **kw):
                    if self._pending is not None and (kw.get("accum_out") is not None or "dma" in name):
                        self._e.wait_ge(*self._pending)
                        self._pending = None
                    res = real(*a, **kw)
                    if self._pending is not None:
                        res._wait_ge(*self._pending)
                        self._pending = None
                    return res
                return call

        def replay(eng_rec, attach=False):
            def run(e):
                items = eng_rec.items
                prox = FirstWait(e) if attach else None
                for idx, it in enumerate(items):
                    if it[0] == "w":
                        if attach and idx + 1 < len(items) and items[idx + 1][0] == "o":
                            prox._pending = (sems[it[1]], it[2])
                        else:
                            e.wait_ge(sems[it[1]], it[2])
                    else:
                        ins = it[1](prox if attach else e)
                        assert not attach or prox._pending is None
                        ins.then_inc(sems[it[2][0]], it[2][1])
            return run

        block.tensor(replay(PE, attach=True))
        block.scalar(replay(ACT, attach=True))
        block.vector(replay(DVE, attach=True))
        block.gpsimd(replay(POOL, attach=True))
        block.sync(replay(SP))
    return nc


_CACHE = {}


def kernel(x_prompt, x_sample, cache_win_k, cache_win_v, rel_bias_table, norm_gains,
           ffn1_w_gate, ffn1_w_up, ffn1_w_down, w_in, attn_sinks, gmlp_ln_g, gmlp_ln_b,
           gmlp_w_s, gmlp_b_s, w_branch_attn, w_branch_gmlp, w_out,
           ffn2_w_gate, ffn2_w_up, ffn2_w_down):
    f = lambda a: np.ascontiguousarray(np.asarray(a, dtype=np.float32))
    if "nc" not in _CACHE:
        _CACHE["nc"] = build_program()
    nc = _CACHE["nc"]
    j = np.arange(384)
    bucket = t5_bucket_np(127 - j)
    oh = np.zeros((32, 384), np.float32)
    oh[bucket, j] = 1.0
    ident = np.eye(128, dtype=np.float32)
    jmat = np.zeros((128, 192), np.float32)
    for m in range(128):
        jmat[127 - m, m] = 1.0
    for m in range(16):
        jmat[127 - m, 128 + m] = 1.0
        jmat[127 - m, 128 + 32 + m] = 1.0
    shared = {
        "tbl": f(rel_bias_table), "gains": f(norm_gains[0]),
        "f1g": f(ffn1_w_gate[0]), "f1u": f(ffn1_w_up[0]), "f1d": f(ffn1_w_down[0]),
        "f2g": f(ffn2_w_gate[0]), "f2u": f(ffn2_w_up[0]), "f2d": f(ffn2_w_down[0]),
        "win": f(w_in[0]), "wba": f(w_branch_attn[0]), "wbg": f(w_branch_gmlp[0]), "wo": f(w_out[0]),
        "sinks": f(attn_sinks[0]).reshape(1, 16), "lng": f(gmlp_ln_g[0]).reshape(1, D), "lnb": f(gmlp_ln_b[0]).reshape(1, D),
        "wsd": f(gmlp_w_s[0]), "bsd": f(gmlp_b_s[0]).reshape(1, 512), "identd": ident, "ohd": oh, "jd": jmat,
    }
    xp = f(x_prompt)
    xs = f(x_sample)
    ckk = f(cache_win_k[0]).reshape(16, 128, 128)
    cvv = f(cache_win_v[0]).reshape(16, 128, 128)
    in_maps = []
    for c in range(8):
        m = dict(shared)
        m["xp"] = xp[c]
        m["xs"] = np.ascontiguousarray(xs[2 * c:2 * c + 2].reshape(32, D))
        m["ck"] = np.ascontiguousarray(ckk[2 * c:2 * c + 2])
        m["cv"] = np.ascontiguousarray(cvv[2 * c:2 * c + 2])
        in_maps.append(m)
    res = run_bass_kernel_spmd(nc, in_maps, core_ids=list(range(8)))
    R = res.results
    y_prompt = np.stack([np.asarray(r["yp"]) for r in R], 0).astype(np.float32)
    y_sample = np.concatenate([np.asarray(r["ys"]).reshape(2, 16, D) for r in R], 0).astype(np.float32)
    win_k = np.stack([np.asarray(r["wkp"]).reshape(128, 2, 64) for r in R], 0)[None].astype(np.float32)
    win_v = np.stack([np.asarray(r["wvp"]).reshape(128, 2, 64) for r in R], 0)[None].astype(np.float32)
    new_k = np.concatenate([np.asarray(r["nks"]).reshape(2, 16, 2, 64) for r in R], 0)[None].astype(np.float32)
    new_v = np.concatenate([np.asarray(r["nvs"]).reshape(2, 16, 2, 64) for r in R], 0)[None].astype(np.float32)
    gv = np.concatenate([np.asarray(r["gvs"]).reshape(2, 16, D) for r in R], 0)[None].astype(np.float32)
    return (y_prompt, y_sample, win_k, win_v, new_k, new_v, gv)
```

```python
import math
import types
from contextlib import ExitStack

import numpy as np
import concourse.bass as bass
import concourse.mybir as mybir
from concourse.bass_utils import run_bass_kernel_spmd

F32 = mybir.dt.float32
BF16 = mybir.dt.bfloat16
I32 = mybir.dt.int32
AF = mybir.ActivationFunctionType
ALU = mybir.AluOpType

D = 1024
DFF = 2816
NFC = 22
INW = 5376
SEQ = 4096
TT = 512
NPT = SEQ // TT
EPS = 1e-6
NEG = -30000.0
NSLOT = 7
NCONV = 8
MAGIC = float(0x5F3759DF)


class Ev:
    __slots__ = ("sem", "val")

    def __init__(self, sem, val):
        self.sem = sem
        self.val = val


class DmaSem:
    def __init__(self, key):
        self.key = key
        self.count = 0


class Tracker:
    def __init__(self):
        self.lw = {}
        self.rd = {}


def freeze(fn):
    if fn.__closure__ is None:
        return fn
    cells = []
    for c in fn.__closure__:
        try:
            cells.append(types.CellType(c.cell_contents))
        except ValueError:
            cells.append(c)
    return types.FunctionType(fn.__code__, fn.__globals__, fn.__name__, fn.__defaults__, tuple(cells))


class Eng:
    def __init__(self, name, tr, sync_self=True):
        self.name = name
        self.tr = tr
        self.items = []
        self.count = 0
        self.waited = {}
        self.sync_self = sync_self

    def emit(self, fn, reads=(), writes=(), dma=None):
        tr = self.tr
        deps = {}

        def add(ev):
            if ev is not None and deps.get(ev.sem, 0) < ev.val:
                deps[ev.sem] = ev.val

        for c in reads:
            add(tr.lw.get(c))
        for c in writes:
            add(tr.lw.get(c))
            for ev in tr.rd.get(c, {}).values():
                add(ev)
        for k, v in deps.items():
            if k == self.name and not self.sync_self:
                continue
            if self.waited.get(k, 0) < v:
                self.items.append(("w", k, v))
                self.waited[k] = v
        if dma is None:
            self.count += 1
            ev = Ev(self.name, self.count)
            inc = (self.name, 1)
        else:
            dma.count += 16
            ev = Ev(dma.key, dma.count)
            inc = (dma.key, 16)
        self.items.append(("o", freeze(fn), inc))
        for c in reads:
            tr.rd.setdefault(c, {})[ev.sem] = ev
        for c in writes:
            tr.lw[c] = ev
            tr.rd[c] = {}
        return ev

    def wait_ev(self, ev):
        if self.waited.get(ev.sem, 0) < ev.val:
            self.items.append(("w", ev.sem, ev.val))
            self.waited[ev.sem] = ev.val


def t5_bucket_np(rel):
    half = 16
    max_exact = 8
    ret = np.where(rel > 0, half, 0)
    n = np.abs(rel)
    nf = np.maximum(n, 1).astype(np.float32)
    large = max_exact + (np.log(nf / np.float32(max_exact)) / np.float32(math.log(128 / max_exact))
                         * np.float32(half - max_exact)).astype(np.int32)
    large = np.minimum(large, half - 1)
    return ret + np.where(n < max_exact, n, large)


def piece_list():
    pcs = []
    for f in (1, 2):
        ffn = []
        for j in range(11):
            ffn.append(("g%d" % f, j))
            ffn.append(("u%d" % f, j))
        for j in range(11):
            ffn.append(("d%d" % f, j))
        if f == 1:
            pcs += ffn
            pcs.append(("kv", 0))
            for nm in ("q", "gv", "u"):
                for j in range(4):
                    pcs.append((nm, j))
            for j in range(4):
                pcs.append(("ga", j))
                pcs.append(("gb", j))
            for j in range(4):
                pcs.append(("ba", j))
                pcs.append(("bg", j))
            for j in range(4):
                pcs.append(("o", j))
        else:
            pcs += ffn
    return pcs


PIECES = piece_list()
NPIECE = len(PIECES)


def build_program():
    nc = bass.Bass("TRN2", target_bir_lowering=False)

    def din(name, shape, dt=F32):
        return nc.dram_tensor(name, list(shape), dt, kind="ExternalInput").ap()

    def dout(name, shape, dt=F32):
        return nc.dram_tensor(name, list(shape), dt, kind="ExternalOutput").ap()

    xp = din("xp", [SEQ, D])
    xs = din("xs", [32, D])
    ck = din("ck", [2, 128, 128])
    cv = din("cv", [2, 128, 128])
    tbl = din("tbl", [32, 16])
    gains = din("gains", [6, D])
    W = {
        "g1": din("f1g", [D, DFF]), "u1": din("f1u", [D, DFF]), "d1": din("f1d", [DFF, D]),
        "g2": din("f2g", [D, DFF]), "u2": din("f2u", [D, DFF]), "d2": din("f2d", [DFF, D]),
        "win": din("win", [D, INW]), "ba": din("wba", [D, D]), "bg": din("wbg", [D, D]), "o": din("wo", [D, D]),
    }
    sinks = din("sinks", [1, 16])
    lng = din("lng", [1, D])
    lnb = din("lnb", [1, D])
    wsd = din("wsd", [4, 128, 128])
    bsd = din("bsd", [1, 512])
    identd = din("identd", [128, 128])
    ohd = din("ohd", [32, 384])
    jd = din("jd", [128, 192])

    yp = dout("yp", [SEQ, D])
    ys = dout("ys", [32, D])
    wkp = dout("wkp", [128, 128])
    wvp = dout("wvp", [128, 128])
    nks = dout("nks", [32, 128])
    nvs = dout("nvs", [32, 128])
    gvs = dout("gvs", [32, D])

    wsc = nc.dram_tensor("wsc", [NPIECE, 128, 2048], BF16, kind="Internal").ap()
    tsc = nc.dram_tensor("tsc", [16, 384], F32, kind="Internal").ap()

    tr = Tracker()
    PE = Eng("PE", tr, sync_self=False)
    ACT = Eng("ACT", tr)
    DVE = Eng("DVE", tr)
    POOL = Eng("POOL", tr)
    SP = Eng("SP", tr)
    dsem = {}

    def ds(key):
        if key not in dsem:
            dsem[key] = DmaSem(key)
        return dsem[key]

    with ExitStack() as es:
        def sb(name, shape, dt):
            return es.enter_context(nc.sbuf_tensor(name, list(shape), dt))

        ps = es.enter_context(nc.psum_tensor("ps", [128, 4096], F32))

        identf = sb("identf", [128, 128], F32)
        identb = sb("identb", [128, 128], BF16)
        gb = sb("gb", [128, 3, D], F32)
        gcol = sb("gcol", [128, 3, 8], F32)
        lngb = sb("lngb", [128, D], F32)
        lnbb = sb("lnbb", [128, D], F32)
        bsb = sb("bsb", [128, 4, 128], F32)
        esink = sb("esink", [128, 16], F32)
        wsT = sb("wsT", [128, 4, 128], BF16)
        wsTs = sb("wsTs", [64, 4, 16], BF16)
        kTc = sb("kTc", [128, 2, 2, 128], BF16)
        Vc = sb("Vc", [128, 2, 2, 65], BF16)
        biasHL = sb("biasHL", [128, 2, 2, 16, 128], BF16)
        biasSHL = sb("biasSHL", [64, 2, 16, 16], BF16)
        xbuf = sb("xbuf", [128, 2, 4, D], F32)
        xsb = sb("xsb", [64, D], F32)
        hTs = [sb("hT0", [128, 8, TT], BF16), sb("hT1", [128, 8, TT], BF16), sb("hTS", [128, 8, 64], BF16)]
        hbuf = sb("hbuf", [128, 4, D], BF16)
        ring = sb("ring", [128, NSLOT, 2048], BF16)
        kT = sb("kT", [128, 2, 640], BF16)
        kTs = sb("kTs", [128, 2, 64], BF16)
        Vaug = sb("Vaug", [128, 5, 2, 65], BF16)
        Vnew = sb("Vnew", [64, 2, 65], BF16)
        kvst = sb("kvst", [128, 256], F32)
        bufABC = sb("bufABC", [128, 24, TT], BF16)
        bufA = bufABC[:, 0:8, :]
        bufB = bufABC[:, 8:16, :]
        bufC = bufABC[:, 16:24, :]
        act = bufABC[:, 0:NFC, :]
        bufD = sb("bufD", [128, 4, D], BF16)
        bufE = sb("bufE", [128, 8, TT], BF16)
        scr = bufD[:].rearrange("p a b -> p (a b)").bitcast(F32)
        oh_sb = scr[0:32, 0:384]
        tst = scr[0:16, 384:768]
        ws_sb = scr[:, 768:1280].rearrange("p (g j) -> p g j", g=4)
        ck_sb = scr[:, 1280:1536].rearrange("p (s f) -> p s f", s=2)
        cv_sb = scr[:, 1536:1792].rearrange("p (s f) -> p s f", s=2)
        jsb = scr[:, 1792:1984]
        tbl_sb = scr[0:32, 1984:2000]
        wsTs_f = sb("wsTs_f", [64, 4, 16], F32)
        XR = [("bufD", t_) for t_ in range(4)]
        PT = sb("PT", [128, 2, 2, 4, 128], BF16)
        obuf = sb("obuf", [128, 2, 16, 64], BF16)
        stmp = sb("stmp", [128, 4, 512], F32)
        lnp = sb("lnp", [128, 4, 2], F32)
        mv4 = sb("mv4", [128, 4, 2], F32)
        sgt = sb("sgt", [128, 2, 512], F32)
        tmpA = sb("tmpA", [128, D], F32)
        tmpB = sb("tmpB", [128, D], F32)
        junk = tmpB[:].bitcast(BF16)[:, 0:D]
        ssq = sb("ssq", [128, 4], F32)
        rstd = sb("rstd", [128, 4], F32)
        nwv = sb("nwv", [128, 4], F32)
        nwi = sb("nwi", [128, 4], I32)
        nwy = sb("nwy", [128, 4], I32)
        nwt = sb("nwt", [128, 4], F32)
        bnst = sb("bnst", [128, 2, 6], F32)
        mv = sb("mv", [128, 2], F32)
        den = sb("den", [128, 4], F32)
        rden = sb("rden", [128, 4], F32)

        qT = sgaT = bufA
        AT = bufB
        uT = sgbT = bufC
        gmT = bufE

        def bank(b):
            return ps[:, b * 512:(b + 1) * 512]

        def bankbf(b):
            return ps[:, b * 512:(b + 1) * 512].bitcast(BF16)

        def pcell(b, q0=0, q1=4):
            return [("ps", b)]

        init_evs = []
        dinit = ds("dinit")

        def init_load(out_ap, in_ap, cells, slow=False):
            def fn(e, o=out_ap, i=in_ap, s=slow):
                if s:
                    return e.dma_start(out=o, in_=i, allow_slow_non_contiguous=True)
                return e.dma_start(out=o, in_=i)
            ev = SP.emit(fn, reads=(), writes=cells, dma=dinit)
            init_evs.append(ev)

        init_load(identf[:], identd, ["identf"])
        init_load(gb[:], bass.AP(tensor=gains.tensor, offset=D, ap=[[0, 128], [2 * D, 3], [1, D]]), ["gb"])
        for i3 in range(3):
            init_load(gcol[:, i3, :], gains[2 * i3, :].rearrange("(kc p) -> p kc", p=128), ["gcol"] if i3 == 0 else [("gcolx", i3)], slow=True)
        init_load(lngb[:], lng.to_broadcast([128, D]), ["lngb"])
        init_load(lnbb[:], lnb.to_broadcast([128, D]), ["lnbb"])
        init_load(bsb[:].rearrange("p g i -> p (g i)"), bsd.to_broadcast([128, 512]), ["bsb"])
        init_load(esink[:], sinks.to_broadcast([128, 16]), ["esink"])
        init_load(tbl_sb[:], tbl, ["tbl_sb"])
        init_load(oh_sb[:], ohd, ["oh_sb"])
        init_load(jsb[:], jd, ["jsb"])
        init_load(ws_sb[:], wsd.rearrange("g i j -> i g j"), ["ws_sb"])
        init_load(ck_sb[:], ck.rearrange("s k f -> k s f"), ["ck_sb"])
        init_load(cv_sb[:], cv.rearrange("s k f -> k s f"), ["cv_sb"])
        for half in (0, 32):
            for g in range(4):
                init_load(wsTs_f[half:half + 16, g, :], wsd[g, 0:16, 0:16].rearrange("i j -> j i"),
                          ["wsTs_f%d" % half] if g == 0 else [("wsTs_fx", half, g)], slow=True)
        for ev in init_evs:
            ev.val = dinit.count

        def x_load(t):
            xb = t % 2
            POOL.emit(lambda e: e.dma_start(out=xbuf[:, xb, :, :],
                                            in_=xp[t * TT:(t + 1) * TT, :].rearrange("(tt p) d -> p tt d", p=128)),
                      writes=[("x", xb, tt) for tt in range(4)], dma=ds("xl%d" % xb))

        DVE.emit(lambda e: e.memset(xsb[:], 0.0), writes=[("xs", 0), ("xs1",)])
        POOL.emit(lambda e: e.dma_start(out=xsb[0:16, :], in_=xs[0:16, :]), writes=[("xs", 0)], dma=ds("xl_s0"))
        POOL.emit(lambda e: e.dma_start(out=xsb[32:48, :], in_=xs[16:32, :]), writes=[("xs1",)], dma=ds("xl_s1"))
        x_load(0)
        if NPT > 1:
            x_load(1)

        def conv_aps(i):
            kind, j = PIECES[i]
            dst = wsc[i]
            if kind in ("g1", "u1", "g2", "u2"):
                return [(dst.rearrange("p (kc n) -> p kc n", kc=8),
                         W[kind][:, j * 256:(j + 1) * 256].rearrange("(kc p) n -> p kc n", p=128))]
            if kind in ("d1", "d2"):
                return [(dst.rearrange("p (fc n) -> p fc n", fc=2),
                         W[kind][j * 256:(j + 1) * 256, :].rearrange("(fc p) n -> p fc n", p=128))]
            if kind in ("ba", "bg"):
                return [(dst.rearrange("p (kc n) -> p kc n", kc=8),
                         W[kind][:, j * 256:(j + 1) * 256].rearrange("(kc p) n -> p kc n", p=128))]
            if kind in ("o", "gv"):
                src = W["o"] if kind == "o" else W["win"]
                c0 = (j // 2) * 512 + (0 if kind == "o" else 2304)
                k0 = (j % 2) * 4
                return [(dst.rearrange("p (kc n) -> p kc n", kc=4),
                         src[:, c0:c0 + 512].rearrange("(kc p) n -> p kc n", p=128)[:, k0:k0 + 4, :])]
            win = W["win"]
            if kind == "q":
                res = []
                d3 = dst.rearrange("p (kc n) -> p kc n", kc=8)
                for lc in range(2):
                    c = 2 * j + lc
                    for g in range(2):
                        col = (g * 8 + c) * 64
                        res.append((d3[:, :, lc * 128 + g * 64:lc * 128 + (g + 1) * 64],
                                    win[:, col:col + 64].rearrange("(kc p) n -> p kc n", p=128)))
                return res
            base = {"kv": 1024, "u": 1280, "gv": 2304, "ga": 3328, "gb": 4352}[kind]
            c0 = base + j * 256
            return [(dst.rearrange("p (kc n) -> p kc n", kc=8),
                     win[:, c0:c0 + 256].rearrange("(kc p) n -> p kc n", p=128))]

        nconv = 0
        wsc_events = {}
        for i in range(NPIECE):
            for (d_ap, s_ap) in conv_aps(i):
                k = nconv % NCONV
                nconv += 1
                ev = POOL.emit(lambda e, d_ap=d_ap, s_ap=s_ap: e.dma_start(out=d_ap, in_=s_ap),
                               writes=[("convsem", k)], dma=ds("conv%d" % k))
                wsc_events.setdefault(i, []).append(ev)

        state = {"n": 0, "cur": None}
        PHASE_BASE = {"F1": 0, "M": 33, "F2": 66}

        def begin_phase(ph):
            state["cur"] = PHASE_BASE[ph]
            state["end"] = PHASE_BASE[ph] + 33

        def next_piece(expect):
            i = state["cur"]
            state["cur"] += 1
            assert i < state["end"]
            kind = PIECES[i][0]
            assert kind.rstrip("12") == expect.rstrip("12"), (PIECES[i], expect)
            slot = state["n"] % NSLOT
            state["n"] += 1
            assert slot not in state.get("pinned", ()), "ring slot still pinned"
            for ev in wsc_events[i]:
                SP.wait_ev(ev)
            SP.emit(lambda e: e.dma_start(out=ring[:, slot, :], in_=wsc[i]),
                    writes=[("ring", slot)], dma=ds("ring%d" % slot))
            return slot

        DVE.emit(lambda e: e.tensor_copy(out=identb[:], in_=identf[:]), reads=["identf"], writes=["identb"])
        ACT.emit(lambda e: e.activation(out=esink[:], in_=esink[:], func=AF.Exp), reads=[], writes=["esink"])
        def late_init():
            def head():
                PE.emit(lambda e: e.matmul(out=ps[0:16, 0:384], lhsT=tbl_sb[:], rhs=oh_sb[:], start=True, stop=True),
                        reads=["tbl_sb", "oh_sb"] + XR, writes=pcell(0))
                DVE.emit(lambda e: e.tensor_copy(out=tst[:], in_=ps[0:16, 0:384]), reads=pcell(0) + XR, writes=["tst"])
                SP.emit(lambda e: e.dma_start(out=tsc, in_=tst[:]), reads=["tst"] + XR, writes=["tsc"], dma=ds("tsc"))
            brt = bufE[:].rearrange("p a b -> p (a b)").bitcast(F32)
            cells = [("bufE", c_) for c_ in range(8)]

            def br_dma(kind):
                offp = 128 * (1 - kind)
                src = bass.AP(tensor=tsc.tensor, offset=offp, ap=[[1, 128], [384, 16], [1, 128]])
                SP.emit(lambda e: e.dma_start(out=brt[:, 0:2048].rearrange("p (h q) -> p h q", h=16), in_=src),
                        reads=["tsc"], writes=cells, dma=ds("bias%d" % kind))

            def br_mm(kind):
                for cc in range(4):
                    PE.emit(lambda e: e.matmul(out=bank(cc), lhsT=jsb[:, 0:128],
                                               rhs=brt[:, cc * 512:(cc + 1) * 512], start=True, stop=True),
                            reads=cells + ["jsb"] + XR, writes=pcell(cc))
                    dH = biasHL[:, 0, kind, :, :].rearrange("p h q -> p (h q)")[:, cc * 512:(cc + 1) * 512]
                    dL = biasHL[:, 1, kind, :, :].rearrange("p h q -> p (h q)")[:, cc * 512:(cc + 1) * 512]
                    DVE.emit(lambda e: e.tensor_copy(out=dH, in_=bank(cc)), reads=pcell(cc), writes=[("bias", kind)])
                    DVE.emit(lambda e: e.tensor_tensor(out=dL, in0=bank(cc), in1=dH, op=ALU.subtract),
                             reads=pcell(cc) + [("bias", kind)], writes=[("bias", kind)])
            def rest():
                PE.emit(lambda e: e.matmul(out=bank(4)[0:64, 0:256].rearrange("p (h q) -> p h q", h=16), lhsT=jsb[:, 128:192],
                                           rhs=brt[:, 0:2048].rearrange("p (h q) -> p h q", h=16)[:, :, 0:16], start=True, stop=True),
                        reads=[("bufE", c_) for c_ in range(8)] + ["jsb"] + XR, writes=pcell(4))
                DVE.emit(lambda e: e.tensor_copy(out=biasSHL[:, 0, :, :].rearrange("p h q -> p (h q)"), in_=bank(4)[0:64, 0:256]),
                         reads=pcell(4), writes=[("biasS", 0), ("biasS", 32)])
                DVE.emit(lambda e: e.tensor_tensor(out=biasSHL[:, 1, :, :].rearrange("p h q -> p (h q)"), in0=bank(4)[0:64, 0:256],
                                                   in1=biasSHL[:, 0, :, :].rearrange("p h q -> p (h q)"), op=ALU.subtract),
                         reads=pcell(4) + [("biasS", 0)], writes=[("biasS", 0), ("biasS", 32)])
                DVE.emit(lambda e: e.memset(biasHL[0:64, 0, 0, :, 64:128], NEG), writes=[("bias", 0)])
                DVE.emit(lambda e: e.memset(biasHL[0:64, 1, 0, :, 64:128], 0.0), writes=[("bias", 0)])
                DVE.emit(lambda e: e.memset(biasHL[64:128, 0, 1, :, 0:64], NEG), writes=[("bias", 1)])
                DVE.emit(lambda e: e.memset(biasHL[64:128, 1, 1, :, 0:64], 0.0), writes=[("bias", 1)])
                DVE.emit(lambda e: e.memset(ws_sb[0:64, :, 64:128], 0.0), reads=XR, writes=["ws_sb"])
                for g in range(4):
                    PE.emit(lambda e, g=g: e.transpose(out=ps[:, 512 + g * 128:512 + (g + 1) * 128], in_=ws_sb[:, g, :],
                                                       identity=identf[:]),
                            reads=["ws_sb", "identf"] + XR, writes=[("ps", 1)])
                DVE.emit(lambda e: e.tensor_copy(out=wsT[:].rearrange("p g i -> p (g i)"), in_=bank(1)),
                         reads=pcell(1), writes=["wsT"])
                DVE.emit(lambda e: e.tensor_copy(out=wsTs[0:16, :, :], in_=wsTs_f[0:16, :, :]), reads=["wsTs_f0"], writes=["wsTs0"])
                DVE.emit(lambda e: e.tensor_copy(out=wsTs[32:48, :, :], in_=wsTs_f[32:48, :, :]), reads=["wsTs_f32"], writes=["wsTs32"])
                for s in range(2):
                    PE.emit(lambda e, s=s: e.transpose(out=ps[:, 1024 + s * 128:1024 + (s + 1) * 128], in_=ck_sb[:, s, :],
                                                       identity=identf[:]),
                            reads=["ck_sb", "identf"] + XR, writes=[("ps", 2)])
                DVE.emit(lambda e: e.memset(kTc[:], 0.0), writes=["kTc"])
                DVE.emit(lambda e: e.memset(kT[:], 0.0), writes=[("kT", b_) for b_ in range(5)])
                DVE.emit(lambda e: e.memset(kTs[:], 0.0), writes=["kTs"])
                for k_ in range(2):
                    DVE.emit(lambda e: e.tensor_copy(out=kTc[k_ * 64:(k_ + 1) * 64, :, k_, :],
                                                     in_=ps[k_ * 64:(k_ + 1) * 64, 1024:1280].rearrange("p (s k) -> p s k", s=2)),
                             reads=[("ps", 2)], writes=["kTc"])
                DVE.emit(lambda e: e.memset(Vc[:, :, :, 64:65], 1.0), writes=["Vc1"])
                DVE.emit(lambda e: e.tensor_copy(out=Vc[:, :, :, 0:64],
                                                 in_=cv_sb[:].rearrange("p s (k d) -> p s k d", k=2)),
                         reads=["cv_sb"] + XR, writes=["Vc"])
                DVE.emit(lambda e: e.memset(Vaug[:, :, :, 64:65], 1.0), writes=["Vaug1"])
                DVE.emit(lambda e: e.memset(Vnew[:, :, 64:65], 1.0), writes=["Vnew1"])
                DVE.emit(lambda e: e.memset(bufE[:], 0.0), writes=[("bufE", c_) for c_ in range(8)])

            return [lambda: (head(), br_dma(0)), lambda: (br_mm(0), br_dma(1)), lambda: (br_mm(1), rest())]

        def newton(P, n, scale, eps=EPS):
            v, ti, yi, t = nwv[:P, 0:n], nwi[:P, 0:n], nwy[:P, 0:n], nwt[:P, 0:n]
            y = nwy[:P, 0:n].bitcast(F32)
            DVE.emit(lambda e: e.tensor_scalar(out=v, in0=ssq[:P, 0:n], scalar1=scale, scalar2=eps,
                                               op0=ALU.mult, op1=ALU.add), reads=["ssq"], writes=["nwv"])
            DVE.emit(lambda e: e.tensor_single_scalar(out=ti, in_=nwv[:P, 0:n].bitcast(I32), scalar=1,
                                                      op=ALU.arith_shift_right), reads=["nwv"], writes=["nwi"])
            DVE.emit(lambda e: e.tensor_scalar(out=yi, in0=ti, scalar1=-1.0, scalar2=MAGIC,
                                               op0=ALU.mult, op1=ALU.add), reads=["nwi"], writes=["nwy"])
            DVE.emit(lambda e: e.tensor_scalar(out=v, in0=v, scalar1=-0.5, scalar2=None, op0=ALU.mult),
                     reads=["nwv", "nwi"], writes=["nwv"])
            for it in range(3):
                DVE.emit(lambda e: e.tensor_tensor(out=t, in0=y, in1=y, op=ALU.mult), reads=["nwy"], writes=["nwt"])
                DVE.emit(lambda e: e.tensor_tensor(out=t, in0=t, in1=v, op=ALU.mult), reads=["nwt", "nwv"], writes=["nwt"])
                if it < 2:
                    DVE.emit(lambda e: e.scalar_tensor_tensor(out=y, in0=t, scalar=1.5, in1=y, op0=ALU.add, op1=ALU.mult),
                             reads=["nwy", "nwt"], writes=["nwy"])
                else:
                    DVE.emit(lambda e: e.scalar_tensor_tensor(out=rstd[:P, 0:n], in0=t, scalar=1.5, in1=y, op0=ALU.add, op1=ALU.mult),
                             reads=["nwy", "nwt"], writes=["rstd"])

        class Tile:
            pass

        def prenorm_A1(tl):
            P = tl.P
            for tt in range(tl.ntt):
                ACT.emit(lambda e: e.activation(out=junk[:P, :], in_=tl.xap(tt), func=AF.Square,
                                                accum_out=ssq[:P, tt:tt + 1]),
                         reads=tl.xcells(tt), writes=["tmpB", "ssq"])
            newton(P, tl.ntt, 1.0 / D)

        def prenorm_A2(tl):
            P = tl.P
            for tt in range(tl.ntt):
                ACT.emit(lambda e: e.activation(out=hbuf[:P, tt, :], in_=tl.xap(tt), func=AF.Copy,
                                                scale=rstd[:P, tt:tt + 1]),
                         reads=tl.xcells(tt) + ["rstd"], writes=[("hbuf", tt)])

        def prenorm_B(tl, gi, hi):
            P = tl.P
            hT = hTs[hi]
            for tt in range(tl.ntt):
                b = 4 + tt

                def trs(e):
                    ins = None
                    o = bankbf(b).rearrange("p (c t) -> p c t", c=8)
                    for kc in range(8):
                        ins = e.transpose(out=o[:, kc, 0:P], in_=hbuf[:P, tt, kc * 128:(kc + 1) * 128],
                                          identity=identb[:P, :P])
                    return ins
                PE.emit(trs, reads=[("hbuf", tt), "identb"], writes=pcell(b))
                DVE.emit(lambda e: e.tensor_tensor(
                    out=hT[:, :, tt * 128:tt * 128 + P],
                    in0=bankbf(b).rearrange("p (c t) -> p c t", c=8)[:, :, 0:P],
                    in1=gcol[:, gi // 2, :].unsqueeze(2).to_broadcast([128, 8, P]), op=ALU.mult),
                    reads=pcell(b) + ["gcol", ("gcolx", 1), ("gcolx", 2)], writes=[("hT", hi, tt)])

        def post_norm(tl, gi, factor, pool_x=None):
            if pool_x is None:
                pool_x = tl.use_pool
            P = tl.P
            stmp2 = stmp[:, 0:2, :].rearrange("p a b -> p (a b)")
            T = [(tmpA, ["tmpA"]), (stmp2, [("stmp", 0), ("stmp", 1)])]
            for tt in range(tl.ntt):
                ACT.emit(lambda e: e.activation(out=junk[:P, :], in_=ps[:P, tt * 1024:(tt + 1) * 1024],
                                                func=AF.Square, accum_out=ssq[:P, tt:tt + 1]),
                         reads=pcell(2 * tt) + pcell(2 * tt + 1), writes=["tmpB", "ssq", ("sqdone", tt)])

            def ymul(tt):
                tb, tc = T[tt % 2]
                DVE.emit(lambda e: e.tensor_tensor(out=tb[:P, :], in0=ps[:P, tt * 1024:(tt + 1) * 1024],
                                                   in1=gb[:P, gi // 2, :], op=ALU.mult),
                         reads=pcell(2 * tt) + pcell(2 * tt + 1) + ["gb", ("sqdone", tt)], writes=tc)

            def xupd(tt):
                tb, tc = T[tt % 2]
                if pool_x and tt >= 2:
                    POOL.emit(lambda e: e.tensor_scalar(out=tb[:P, :], in0=tb[:P, :], scalar1=rstd[:P, tt:tt + 1], scalar2=0.0,
                                                        op0=ALU.mult, op1=ALU.add), reads=tc + ["rstd"], writes=tc)
                    POOL.emit(lambda e: e.tensor_tensor(out=tl.xap(tt), in0=tb[:P, :], in1=tl.xap(tt), op=ALU.add),
                              reads=tc + tl.xcells(tt), writes=[tl.xcells(tt)[0]])
                    return
                DVE.emit(lambda e: e.scalar_tensor_tensor(
                    out=tl.xap(tt), in0=tb[:P, :], scalar=rstd[:P, tt:tt + 1], in1=tl.xap(tt),
                    op0=ALU.mult, op1=ALU.add),
                    reads=tc + ["rstd"] + tl.xcells(tt), writes=[tl.xcells(tt)[0]])

            for tt in range(min(2, tl.ntt)):
                ymul(tt)
            newton(P, tl.ntt, 1.0 / (D * factor * factor), eps=EPS / (factor * factor))
            for tt in range(tl.ntt):
                if tt >= 2:
                    ymul(tt)
                xupd(tt)

        def fm_group(slot, lc, rhs_buf, b, NC, reads):
            def fn(e):
                ins = None
                for kc in range(8):
                    ins = e.matmul(out=bank(b)[:, 0:NC], lhsT=ring[:, slot, kc * 256 + lc * 128:kc * 256 + (lc + 1) * 128],
                                   rhs=rhs_buf[:, kc, 0:NC], start=(kc == 0), stop=(kc == 7))
                return ins
            PE.emit(fn, reads=[("ring", slot)] + reads, writes=pcell(b))

        def tm_group(slot, lhs_buf, tt, P, b, col0, reads, wcells):
            def fn(e):
                ins = None
                for kc in range(8):
                    ins = e.matmul(out=bank(b)[:P, col0:col0 + 256], lhsT=lhs_buf[:, kc, tt * 128:tt * 128 + P],
                                   rhs=ring[:, slot, kc * 256:(kc + 1) * 256], start=(kc == 0), stop=(kc == 7))
                return ins
            PE.emit(fn, reads=[("ring", slot)] + reads, writes=wcells)

        hT_cells = lambda tl, hi: [("hT", hi, tt) for tt in range(tl.ntt)]

        def ffn(tl, f, gi_post, hi, hooks, early=None, co=None):
            P, NC = tl.P, tl.NC
            hT = hTs[hi]
            begin_phase("F%d" % f)
            if co is not None:
                actS = PT[:].rearrange("p a b c d -> p (a b c d)")[:, 0:NFC * 64].rearrange("p (f t) -> p f t", f=NFC)
                ptc = [("PT", a_, b_) for a_ in range(2) for b_ in range(2)]
                coh = hT_cells(co, 2)
            for j in range(11):
                if early is not None and j in early:
                    early[j]()
                if j in (6, 8, 10):
                    hooks[(j - 6) // 2]()
                sg_ = next_piece("g%d" % f)
                su_ = next_piece("u%d" % f)
                for lc in range(2):
                    fc = 2 * j + lc
                    bg_, bu_ = (fc % 2) * 2, (fc % 2) * 2 + 1
                    fm_group(sg_, lc, hT, bg_, NC, hT_cells(tl, hi))
                    fm_group(su_, lc, hT, bu_, NC, hT_cells(tl, hi))
                    sp_ = fc % 2
                    ACT.emit(lambda e, bg_=bg_, sp_=sp_: e.activation(out=sgt[:, sp_, 0:NC], in_=bank(bg_)[:, 0:NC], func=AF.Silu),
                             reads=pcell(bg_), writes=[("sgt", sp_)])
                    DVE.emit(lambda e, bu_=bu_, sp_=sp_, fc=fc: e.tensor_tensor(out=act[:, fc, 0:NC], in0=bank(bu_)[:, 0:NC],
                                                                               in1=sgt[:, sp_, 0:NC], op=ALU.mult),
                             reads=pcell(bu_) + [("sgt", sp_)], writes=[("abc", fc)])
                    if co is not None:
                        fm_group(sg_, lc, hTs[2], 4, 64, coh)
                        fm_group(su_, lc, hTs[2], 5, 64, coh)
                        ACT.emit(lambda e: e.activation(out=stmp[:, 2 + sp_, 0:64], in_=bank(4)[:, 0:64], func=AF.Silu),
                                 reads=pcell(4), writes=[("stmp", 2 + sp_)])
                        DVE.emit(lambda e: e.tensor_tensor(out=actS[:, fc, :], in0=bank(5)[:, 0:64], in1=stmp[:, 2 + sp_, 0:64],
                                                           op=ALU.mult), reads=pcell(5) + [("stmp", 2 + sp_)], writes=ptc)
            for j in range(11):
                sd_ = next_piece("d%d" % f)
                for tt in range(tl.ntt):
                    for half in range(2):
                        b = tt * 2 + half

                        def fn(e, j=j, tt=tt, half=half, b=b, sd_=sd_):
                            ins = None
                            for lc in range(2):
                                ins = e.matmul(out=bank(b)[:P, :], lhsT=act[:, 2 * j + lc, tt * 128:tt * 128 + P],
                                               rhs=ring[:, sd_, lc * 1024 + half * 512:lc * 1024 + (half + 1) * 512],
                                               start=(j == 0 and lc == 0), stop=(j == 10 and lc == 1))
                            return ins
                        PE.emit(fn, reads=[("ring", sd_), ("abc", 2 * j), ("abc", 2 * j + 1)], writes=pcell(b))
            post_norm(tl, gi_post, 0.5, pool_x=(tl.use_pool and not (f == 1 and tl.t <= 1)))
            if co is not None:
                state["cur"] = PHASE_BASE["F%d" % f] + 22
                for j in range(11):
                    sd_ = next_piece("d%d" % f)
                    for half in range(2):
                        def fn2(e):
                            ins = None
                            for lc in range(2):
                                ins = e.matmul(out=bank(half)[:64, :], lhsT=actS[:, 2 * j + lc, :],
                                               rhs=ring[:, sd_, lc * 1024 + half * 512:lc * 1024 + (half + 1) * 512],
                                               start=(j == 0 and lc == 0), stop=(j == 10 and lc == 1))
                            return ins
                        PE.emit(fn2, reads=[("ring", sd_)] + ptc, writes=pcell(half))
                post_norm(co, gi_post, 0.5, pool_x=False)

        def mixer(tl, hi, hooks):
            P, NC = tl.P, tl.NC
            samp = tl.sample
            hT = hTs[hi]
            begin_phase("M")
            hc = hT_cells(tl, hi)
            EW = POOL if tl.use_pool else DVE
            vn = bufD
            skv = next_piece("kv")
            fm_group(skv, 0, hT, 0, NC, hc)
            if samp:
                for k_ in range(2):
                    ACT.emit(lambda e: e.copy(out=kTs[k_ * 64:(k_ + 1) * 64, k_, :], in_=bank(0)[k_ * 64:(k_ + 1) * 64, 0:64]),
                             reads=pcell(0), writes=["kTs"])
            else:
                for k_ in range(2):
                    ACT.emit(lambda e: e.copy(out=kT[k_ * 64:(k_ + 1) * 64, k_, 128:640], in_=bank(0)[k_ * 64:(k_ + 1) * 64, :]),
                             reads=pcell(0), writes=[("kT", b_) for b_ in range(1, 5)])
            for tt in range(tl.ntt):
                b = 1 + tt % 3
                tm_group(skv, hT, tt, P, b, 0, hc, pcell(b))
                src = bank(b)[:P, 0:256]
                if samp:
                    ACT.emit(lambda e: e.copy(out=Vnew[:, :, 0:64], in_=src[:, 128:256].rearrange("p (k d) -> p k d", k=2)),
                             reads=pcell(b), writes=["Vnew"])
                    ACT.emit(lambda e: e.copy(out=kvst[0:64, :], in_=src), reads=pcell(b), writes=["kvst"])
                    for s_ in range(2):
                        POOL.emit(lambda e: e.dma_start(out=nks[16 * s_:16 * s_ + 16, :], in_=kvst[32 * s_:32 * s_ + 16, 0:128]),
                                  reads=["kvst"], dma=ds("o_nks%d" % s_))
                        POOL.emit(lambda e: e.dma_start(out=nvs[16 * s_:16 * s_ + 16, :], in_=kvst[32 * s_:32 * s_ + 16, 128:256]),
                                  reads=["kvst"], dma=ds("o_nvs%d" % s_))
                else:
                    ACT.emit(lambda e: e.copy(out=Vaug[:, 1 + tt, :, 0:64], in_=src[:, 128:256].rearrange("p (k d) -> p k d", k=2)),
                             reads=pcell(b), writes=[("V", 1 + tt)])
                    if tl.last and tt == 3:
                        ACT.emit(lambda e: e.copy(out=kvst[:, :], in_=src), reads=pcell(b), writes=["kvst"])
                        POOL.emit(lambda e: e.dma_start(out=wkp, in_=kvst[:, 0:128]), reads=["kvst"], dma=ds("o_wkp"))
                        POOL.emit(lambda e: e.dma_start(out=wvp, in_=kvst[:, 128:256]), reads=["kvst"], dma=ds("o_wvp"))
            for j in range(4):
                sq_ = next_piece("q")
                for lc in range(2):
                    c = 2 * j + lc
                    b = c % 4
                    fm_group(sq_, lc, hT, b, NC, hc)
                    ACT.emit(lambda e: e.mul(out=qT[:, c, 0:NC], in_=bank(b)[:, 0:NC], mul=0.125),
                             reads=pcell(b), writes=[("abc", c)])

            if samp:
                units = [(s_, None) for s_ in range(2)]
            else:
                units = [(None, jq) for jq in range(4)]
            Q = 16 if samp else 128
            NQ = 4 * Q
            ginfo = {}

            def att_scores(gi_):
                ui, grp = gi_ // 4, gi_ % 4
                s_, jq = units[ui]
                k = grp // 2
                c0 = (grp % 2) * 4
                h0 = k * 8 + c0
                gpar = gi_ % 2
                if samp:
                    c_lo = 32 * s_
                    rhs_q = qT[:, c0:c0 + 4, c_lo:c_lo + 16]
                    blocks = [dict(lhsT=kTc[:, s_, k, :], M=128, p0=0, pn=128,
                                   idl=identb[:, :], bh=biasHL[:, 0, 0, h0:h0 + 4, 0:16], bl=biasHL[:, 1, 0, h0:h0 + 4, 0:16],
                                   v=Vc[:, s_, k, :], rd=["kTc", ("bias", 0)], vr=["Vc", "Vc1"]),
                              dict(lhsT=kTs[:, k, :], M=64, p0=c_lo, pn=16,
                                   idl=identb[c_lo:c_lo + 16, 0:64], bh=biasSHL[c_lo:c_lo + 16, 0, h0:h0 + 4, :],
                                   bl=biasSHL[c_lo:c_lo + 16, 1, h0:h0 + 4, :], v=Vnew[c_lo:c_lo + 16, k, :],
                                   rd=["kTs", ("biasS", c_lo)], vr=["Vnew", "Vnew1"])]
                else:
                    rhs_q = qT[:, c0:c0 + 4, jq * 128:(jq + 1) * 128]
                    blocks = []
                    for kind, blk in ((0, jq), (1, jq + 1)):
                        if kind == 0 and tl.first and jq == 0:
                            continue
                        blocks.append(dict(lhsT=kT[:, k, blk * 128:(blk + 1) * 128], M=128,
                                           p0=0, pn=128, idl=identb[:, :], bh=biasHL[:, 0, kind, h0:h0 + 4, :],
                                           bl=biasHL[:, 1, kind, h0:h0 + 4, :], v=Vaug[:, blk, k, :],
                                           rd=[("kT", blk), ("bias", kind)], vr=[("V", blk), "Vaug1"]))
                ginfo[gi_] = (blocks, gpar, h0, ui)
                for bi, bk in enumerate(blocks):
                    b = gpar * 2 + bi
                    p0, pn = bk["p0"], bk["pn"]
                    def sfn(e):
                        o = bank(b)[0:bk["M"], 0:NQ].rearrange("p (g q) -> p g q", g=4)
                        e.matmul(out=o, lhsT=bk["lhsT"], rhs=rhs_q, start=True, stop=False)
                        e.matmul(out=o, lhsT=bk["idl"], rhs=bk["bh"], start=False, stop=False)
                        return e.matmul(out=o, lhsT=bk["idl"], rhs=bk["bl"], start=False, stop=True)
                    PE.emit(sfn, reads=[("abc", c) for c in range(c0, c0 + 4)] + bk["rd"] + ["identb"], writes=pcell(b))
                    ACT.emit(lambda e: e.activation(
                        out=PT[p0:p0 + pn, gpar, bi, :, 0:Q],
                        in_=bank(b)[p0:p0 + pn, 0:NQ].rearrange("p (g q) -> p g q", g=4),
                        func=AF.Exp), reads=pcell(b), writes=[("PT", gpar, bi)])

            def att_pv(gi_):
                blocks, gpar, h0, ui = ginfo[gi_]
                grp = gi_ % 4
                ob = ui % 2
                nb = len(blocks)
                bpv = 4

                def pvfn(e):
                    ins = None
                    for g in range(4):
                        for bi, bk in enumerate(blocks):
                            p0, pn = bk["p0"], bk["pn"]
                            ins = e.matmul(out=bank(bpv)[0:Q, g * 65:(g + 1) * 65], lhsT=PT[p0:p0 + pn, gpar, bi, g, 0:Q],
                                           rhs=bk["v"], start=(bi == 0), stop=(bi == nb - 1))
                    return ins
                rds = [("PT", gpar, bi) for bi in range(nb)]
                for bk in blocks:
                    rds += bk["vr"]
                PE.emit(pvfn, reads=rds, writes=pcell(bpv))
                pv3 = bank(bpv)[0:Q, 0:260].rearrange("p (g d) -> p g d", g=4)
                DVE.emit(lambda e: e.tensor_tensor(out=den[0:Q, :], in0=pv3[:, :, 64], in1=esink[0:Q, h0:h0 + 4], op=ALU.add),
                         reads=pcell(bpv) + ["esink"], writes=["den"])
                DVE.emit(lambda e: e.reciprocal(out=rden[0:Q, :], in_=den[0:Q, :]), reads=["den"], writes=["rden"])
                DVE.emit(lambda e: e.tensor_tensor(out=obuf[0:Q, ob, h0:h0 + 4, :], in0=pv3[:, :, 0:64],
                                                   in1=rden[0:Q, :].unsqueeze(2).to_broadcast([Q, 4, 64]), op=ALU.mult),
                         reads=pcell(bpv) + ["rden"], writes=[("obuf", ob, grp)])
            def att_tr(ui):
                if True:
                    ob = ui % 2
                    bt = 5
                    s_, jq = units[ui]

                    def trs(e):
                        ins = None
                        o = bankbf(bt).rearrange("p (c t) -> p c t", c=8)
                        for c in range(8):
                            ins = e.transpose(out=o[:, c, 0:Q], in_=obuf[0:Q, ob, 2 * c:2 * c + 2, :].rearrange("p h d -> p (h d)"),
                                              identity=identb[0:Q, 0:Q])
                        return ins
                    PE.emit(trs, reads=[("obuf", ob, g_) for g_ in range(4)] + ["identb"], writes=pcell(bt))
                    cols = (32 * s_, 32 * s_ + 16) if samp else (jq * 128, (jq + 1) * 128)
                    ACT.emit(lambda e: e.copy(out=AT[:, :, cols[0]:cols[1]],
                                              in_=bankbf(bt).rearrange("p (c t) -> p c t", c=8)[:, :, 0:Q]),
                             reads=pcell(bt), writes=[("abc", 8 + c_) for c_ in range(8)])

            fillers = []
            gvslots = []

            hbuf32 = hbuf[:].rearrange("p a b -> p (a b)").bitcast(F32)

            def tbsel(tt):
                if tt == 0:
                    return tmpB, ["tmpB"]
                if tt == 1:
                    return sgt[:].rearrange("p a b -> p (a b)"), [("sgt", 0), ("sgt", 1)]
                return hbuf32[:, (tt - 2) * 1024:(tt - 1) * 1024], [("hbuf", 2 * (tt - 2)), ("hbuf", 2 * (tt - 2) + 1)]

            def gv_unit(tt):
                gv_unit_a(tt)
                gv_unit_b(tt)

            def gv_unit_a(tt):
                if not gvslots:
                    for j in range(4):
                        gvslots.append(next_piece("gv"))
                tb, tbc = tbsel(tt)
                b0 = 2 * tt
                for half in range(2):
                    def gvfn(e):
                        ins = None
                        for j2 in range(2):
                            for kcl in range(4):
                                kc = j2 * 4 + kcl
                                ins = e.matmul(out=bank(b0 + half)[:P, :], lhsT=hT[:, kc, tt * 128:tt * 128 + P],
                                               rhs=ring[:, gvslots[2 * half + j2], kcl * 512:(kcl + 1) * 512],
                                               start=(kc == 0), stop=(kc == 7))
                        return ins
                    PE.emit(gvfn, reads=[("ring", gvslots[2 * half]), ("ring", gvslots[2 * half + 1])] + hc, writes=pcell(b0 + half))
                ACT.emit(lambda e: e.activation(out=tb[:P, :], in_=ps[:P, b0 * 512:(b0 + 2) * 512], func=AF.Gelu),
                         reads=pcell(b0) + pcell(b0 + 1), writes=tbc)
                for hh in range(2):
                    DVE.emit(lambda e: e.bn_stats(out=bnst[:P, hh, :], in_=tb[:P, hh * 512:(hh + 1) * 512]),
                             reads=tbc, writes=[("bnst", hh)])
                DVE.emit(lambda e: e.bn_aggr(out=mv4[:P, tt, :], in_=bnst[:P, :, :].rearrange("p a b -> p (a b)")),
                         reads=[("bnst", 0), ("bnst", 1)], writes=[("mv4", tt)])
                DVE.emit(lambda e: e.tensor_copy(out=ssq[:P, tt:tt + 1], in_=mv4[:P, tt, 1:2]), reads=[("mv4", tt)], writes=["ssq"])

            def ln_rstd_all():
                n = tl.ntt
                newton(P, n, 1.0)
                DVE.emit(lambda e: e.tensor_copy(out=lnp[:P, 0:n, 0], in_=rstd[:P, 0:n]), reads=["rstd"],
                         writes=[("lnp", t_) for t_ in range(n)])
                DVE.emit(lambda e: e.scalar_tensor_tensor(out=lnp[:P, 0:n, 1], in0=mv4[:P, 0:n, 0], scalar=-1.0, in1=rstd[:P, 0:n],
                                                          op0=ALU.mult, op1=ALU.mult),
                         reads=["rstd"] + [("mv4", t_) for t_ in range(n)], writes=[("lnp", t_) for t_ in range(n)])

            def gv_unit_b(tt):
                tb, tbc = tbsel(tt)
                ACT.emit(lambda e: e.activation(out=tmpA[:P, :], in_=tb[:P, :], func=AF.Identity,
                                                scale=lnp[:P, tt, 0:1], bias=lnp[:P, tt, 1:2]),
                         reads=tbc + [("lnp", tt)], writes=["tmpA"])
                EW.emit(lambda e: e.tensor_tensor(out=tmpA[:P, :], in0=tmpA[:P, :], in1=lngb[:P, :], op=ALU.mult),
                        reads=["tmpA", "lngb"], writes=["tmpA"])
                if samp:
                    EW.emit(lambda e: e.tensor_tensor(out=tmpB[0:64, :], in0=tmpA[:P, :], in1=lnbb[:P, :], op=ALU.add),
                            reads=["tmpA", "lnbb"], writes=["tmpB"])
                    EW.emit(lambda e: e.tensor_copy(out=vn[:P, 0, :], in_=tmpB[0:64, :]), reads=["tmpB"], writes=[("bufD", 0)])
                    for s_ in range(2):
                        SP.emit(lambda e: e.dma_start(out=gvs[16 * s_:16 * s_ + 16, :], in_=tmpB[32 * s_:32 * s_ + 16, :]),
                                reads=["tmpB"], dma=ds("o_gvs%d" % s_))
                else:
                    EW.emit(lambda e: e.tensor_tensor(out=vn[:P, tt, :], in0=tmpA[:P, :], in1=lnbb[:P, :], op=ALU.add),
                            reads=["tmpA", "lnbb"], writes=[("bufD", tt)])

            ustate = {}
            bufE32 = bufE[:].rearrange("p a b -> p (a b)").bitcast(F32)

            def ustage(c):
                if c < 4:
                    return bufE32[:, c * 512:(c + 1) * 512], [("bufE", 2 * c), ("bufE", 2 * c + 1)]
                return stmp[:, c - 4, :], [("stmp", c - 4)]

            def u_chunk(c):
                if c % 2 == 0:
                    ustate["slot"] = next_piece("u")
                b = 6 + c % 2
                fm_group(ustate["slot"], c % 2, hT, b, NC, hc)
                st, stc = ustage(c)
                DVE.emit(lambda e: e.tensor_copy(out=st[:, 0:NC], in_=bank(b)[:, 0:NC]), reads=pcell(b), writes=stc)

            def u_gelu_batch():
                for c in range(8):
                    st, stc = ustage(c)
                    ACT.emit(lambda e: e.activation(out=uT[:, c, 0:NC], in_=st[:, 0:NC], func=AF.Gelu),
                             reads=stc, writes=[("abc", 16 + c)])

            for tt in range(tl.ntt):
                gv_unit_a(tt)
            ln_rstd_all()
            uq = list(range(8))
            for tt in range(tl.ntt):
                for _ in range(2):
                    fillers.append(lambda c=uq.pop(0): u_chunk(c))
                fillers.append(lambda tt=tt: gv_unit_b(tt))
            while uq:
                fillers.append(lambda c=uq.pop(0): u_chunk(c))
            ng = 4 * len(units)
            steps = []
            for i in range(ng):
                steps.append(("S", i))
                if i >= 1:
                    steps.append(("P", i - 1))
                    if (i - 1) % 4 == 3:
                        steps.append(("X", None))
                        steps.append(("T", (i - 1) // 4))
            steps.append(("P", ng - 1))
            steps.append(("T", (ng - 1) // 4))
            for kind, i in steps:
                if kind == "S":
                    att_scores(i)
                elif kind == "P":
                    att_pv(i)
                elif kind == "T":
                    att_tr(i)
                if fillers:
                    fillers.pop(0)()
            while fillers:
                fillers.pop(0)()
            u_gelu_batch()
            if samp:
                DVE.emit(lambda e: e.memset(bufE[:], 0.0), writes=[("bufE", c_) for c_ in range(8)])
            if not samp and not tl.last:
                DVE.emit(lambda e: e.tensor_copy(out=kT[:, :, 0:128], in_=kT[:, :, 512:640]), reads=[("kT", 4)], writes=[("kT", 0)])
                DVE.emit(lambda e: e.tensor_copy(out=Vaug[:, 0, :, 0:64], in_=Vaug[:, 4, :, 0:64]), reads=[("V", 4)], writes=[("V", 0)])
            AT_cells = [("abc", 8 + c_) for c_ in range(8)]

            def gating(c):
                g = c // 2
                b = c % 2
                if samp:
                    for s_ in range(2):
                        c_lo = 32 * s_
                        PE.emit(lambda e: e.matmul(out=bank(b)[:, c_lo:c_lo + 16], lhsT=vn[c_lo:c_lo + 16, 0, c * 128:(c + 1) * 128],
                                                   rhs=wsTs[c_lo:c_lo + 16, g, :], start=True, stop=True),
                                reads=[("bufD", 0), "wsTs%d" % c_lo], writes=pcell(b))
                        DVE.emit(lambda e: e.tensor_tensor(out=stmp[:, b, c_lo:c_lo + 16], in0=bank(b)[:, c_lo:c_lo + 16],
                                                           in1=bsb[:, g, 0:16], op=ALU.add),
                                 reads=pcell(b) + ["bsb"], writes=[("stmp", b)])
                        EW.emit(lambda e: e.tensor_tensor(out=gmT[:, c, c_lo:c_lo + 16], in0=stmp[:, b, c_lo:c_lo + 16],
                                                          in1=uT[:, c, c_lo:c_lo + 16], op=ALU.mult),
                                reads=[("stmp", b), ("abc", 16 + c)], writes=[("bufE", c)])
                else:
                    def gfn(e):
                        ins = None
                        for tt in range(4):
                            ins = e.matmul(out=bank(b)[:, tt * 128:(tt + 1) * 128], lhsT=vn[:, tt, c * 128:(c + 1) * 128],
                                           rhs=wsT[:, g, :], start=True, stop=True)
                        return ins
                    PE.emit(gfn, reads=[("bufD", tt) for tt in range(4)] + ["wsT"], writes=pcell(b))
                    DVE.emit(lambda e: e.tensor_tensor(
                        out=stmp[:, b, :].rearrange("p (t i) -> p t i", t=4), in0=bank(b).rearrange("p (t i) -> p t i", t=4),
                        in1=bsb[:, g, :].unsqueeze(1).to_broadcast([128, 4, 128]), op=ALU.add),
                        reads=pcell(b) + ["bsb"], writes=[("stmp", b)])
                    EW.emit(lambda e: e.tensor_tensor(out=gmT[:, c, :], in0=stmp[:, b, :], in1=uT[:, c, :], op=ALU.mult),
                            reads=[("stmp", b), ("abc", 16 + c)], writes=[("bufE", c)])

            for j in range(4):
                if j == 0:
                    hooks[0]()
                if j == 2:
                    hooks[1]()
                sga_ = next_piece("ga")
                sgb_ = next_piece("gb")
                for lc in range(2):
                    c = 2 * j + lc
                    fm_group(sga_, lc, hT, 2 + c % 2, NC, hc)
                    ACT.emit(lambda e: e.activation(out=sgaT[:, c, 0:NC], in_=bank(2 + c % 2)[:, 0:NC], func=AF.Sigmoid),
                             reads=pcell(2 + c % 2), writes=[("abc", c)])
                for lc in range(2):
                    c = 2 * j + lc
                    gating(c)
                for lc in range(2):
                    c = 2 * j + lc
                    fm_group(sgb_, lc, hT, 4 + c % 2, NC, hc)
                    ACT.emit(lambda e: e.activation(out=sgbT[:, c, 0:NC], in_=bank(4 + c % 2)[:, 0:NC], func=AF.Sigmoid),
                             reads=pcell(4 + c % 2), writes=[("abc", 16 + c)])
            mT = bufD.rearrange("p a b -> p (a b)").rearrange("p (c t) -> p c t", c=8)
            gm_cells = [("bufE", c) for c in range(8)]
            for j in range(4):
                sa_ = next_piece("ba")
                sb_ = next_piece("bg")
                for lc in range(2):
                    oc = 2 * j + lc
                    bA = (oc % 2) * 2
                    bB = bA + 1
                    fm_group(sa_, lc, AT, bA, NC, AT_cells)
                    fm_group(sb_, lc, gmT, bB, NC, gm_cells)
                    if oc % 2 == 0:
                        t1, t2, c1, c2 = stmp[:, 2, 0:NC], stmp[:, 3, 0:NC], ("stmp", 2), ("stmp", 3)
                    else:
                        t1, t2, c1, c2 = sgt[:, 0, 0:NC], sgt[:, 1, 0:NC], ("sgt", 0), ("sgt", 1)
                    DVE.emit(lambda e: e.tensor_tensor(out=t1, in0=bank(bA)[:, 0:NC], in1=sgaT[:, oc, 0:NC], op=ALU.mult),
                             reads=pcell(bA) + [("abc", oc)], writes=[c1])
                    DVE.emit(lambda e: e.tensor_tensor(out=t2, in0=bank(bB)[:, 0:NC], in1=sgbT[:, oc, 0:NC], op=ALU.mult),
                             reads=pcell(bB) + [("abc", 16 + oc)], writes=[c2])
                    EW.emit(lambda e: e.tensor_tensor(out=mT[:, oc, 0:NC], in0=t1, in1=t2, op=ALU.add),
                            reads=[c1, c2], writes=[("bufD", t_) for t_ in range(4)])
            hooks[2]()
            for j in range(4):
                s_o = next_piece("o")
                half, j2 = j // 2, j % 2
                for tt in range(tl.ntt):
                    b = tt * 2 + half

                    def ofn(e):
                        ins = None
                        for kcl in range(4):
                            kc = j2 * 4 + kcl
                            ins = e.matmul(out=bank(b)[:P, :], lhsT=mT[:, kc, tt * 128:tt * 128 + P],
                                           rhs=ring[:, s_o, kcl * 512:(kcl + 1) * 512], start=(kc == 0), stop=(kc == 7))
                        return ins
                    PE.emit(ofn, reads=[("ring", s_o)] + [("bufD", t_) for t_ in range(4)], writes=pcell(b))
            post_norm(tl, 3, 1.0)

        class Tile:
            pass

        def mk_tile(t):
            tl = Tile()
            tl.sample = (t < 0)
            tl.first = (t == 0)
            tl.last = (t == NPT - 1)
            tl.t = t
            tl.use_pool = (t >= 1)
            if tl.sample:
                tl.P, tl.NC, tl.ntt = 64, 64, 1
                tl.xap = lambda tt: xsb[:, :]
                tl.xcells = lambda tt: [("xs", 0), ("xs1",)]
            else:
                xb = t % 2
                tl.P, tl.NC, tl.ntt = 128, TT, 4
                tl.xap = lambda tt: xbuf[:, xb, tt, :]
                tl.xcells = lambda tt: [("x", xb, tt)]
            return tl

        S = mk_tile(-1)
        PT_ = [mk_tile(t) for t in range(NPT)]
        sched = [("F1", PT_[0])]
        if NPT > 1:
            sched.append(("F1", PT_[1]))
        sched += [("M", PT_[0]), ("M", S), ("F2", PT_[0])] if NPT > 1 else [("M", S), ("M", PT_[0]), ("F2", S), ("F2", PT_[0])]
        for n in range(1, NPT):
            sched.append(("M", PT_[n]))
            if n + 1 < NPT:
                sched.append(("F1", PT_[n + 1]))
            else:
                sched.append(("F2", S))
            sched.append(("F2", PT_[n]))
        PRE_GI = {"F1": 0, "M": 2, "F2": 4}

        def finish_tile(tl):
            if tl.sample:
                for s_ in range(2):
                    POOL.emit(lambda e: e.dma_start(out=ys[16 * s_:16 * s_ + 16, :], in_=xsb[32 * s_:32 * s_ + 16, :]),
                              reads=tl.xcells(0), dma=ds("o_ys%d" % s_))
            else:
                t, xb = tl.t, tl.t % 2
                for tt in range(4):
                    POOL.emit(lambda e: e.dma_start(out=yp[t * TT + tt * 128:t * TT + (tt + 1) * 128, :],
                                                    in_=xbuf[:, xb, tt, :]),
                              reads=[("x", xb, tt)], dma=ds("ys%d_%d" % (xb, tt)))
                if t + 2 < NPT:
                    x_load(t + 2)

        ph0, tl0 = sched[0]
        prenorm_A1(S)
        prenorm_A2(S)
        prenorm_B(S, PRE_GI["F1"], 2)
        prenorm_A1(tl0)
        prenorm_A2(tl0)
        prenorm_B(tl0, PRE_GI[ph0], 0)
        li = late_init()
        EARLY = {0: {1: li[0], 8: li[1]}, 1: {2: li[2]}}
        for k, (ph, tl) in enumerate(sched):
            hi = k % 2
            serial_next = False
            if k + 1 < len(sched):
                nph, ntl = sched[k + 1]
                if ntl is tl:
                    serial_next = True
                    hooks = [lambda: None] * 3
                else:
                    hooks = [lambda: prenorm_A1(ntl), lambda: prenorm_A2(ntl), lambda: prenorm_B(ntl, PRE_GI[nph], 1 - hi)]
            else:
                hooks = [lambda: None] * 3
            if ph == "F1":
                ffn(tl, 1, 1, hi, hooks, early=EARLY.get(k), co=(S if k == 0 else None))
            elif ph == "F2":
                ffn(tl, 2, 5, hi, hooks)
                finish_tile(tl)
            else:
                mixer(tl, hi, hooks)
            if serial_next:
                prenorm_A1(ntl)
                prenorm_A2(ntl)
                prenorm_B(ntl, PRE_GI[nph], 1 - hi)
        for key, d in dsem.items():
            if key.startswith("o_") or key.startswith("ys"):
                POOL.wait_ev(Ev(key, d.count))

        sem_keys = ["PE", "ACT", "DVE", "POOL"] + list(dsem.keys())
        sems = {}
        for kname in sem_keys:
            sems[kname] = es.enter_context(nc.semaphore("s_" + kname))
        block = es.enter_context(nc.Block())

        class FirstWait:
            def __init__(self, e):
                self._e = e
                self._pending = None

            def __getattr__(self, name):
                real = getattr(self._e, name)
                if not callable(real):
                    return real

                def call(*a, **kw):
                    if self._pending is not None and (kw.get("accum_out") is not None or "dma" in name):
                        self._e.wait_ge(*self._pending)
                        self._pending = None
                    res = real(*a, **kw)
                    if self._pending is not None:
                        res._wait_ge(*self._pending)
                        self._pending = None
                    return res
                return call

        def replay(eng_rec, attach=False):
            def run(e):
                items = eng_rec.items
                prox = FirstWait(e) if attach else None
                for idx, it in enumerate(items):
                    if it[0] == "w":
                        if attach and idx + 1 < len(items) and items[idx + 1][0] == "o":
                            prox._pending = (sems[it[1]], it[2])
                        else:
                            e.wait_ge(sems[it[1]], it[2])
                    else:
                        ins = it[1](prox if attach else e)
                        assert not attach or prox._pending is None
                        ins.then_inc(sems[it[2][0]], it[2][1])
            return run

        block.tensor(replay(PE, attach=True))
        block.scalar(replay(ACT, attach=True))
        block.vector(replay(DVE, attach=True))
        block.gpsimd(replay(POOL, attach=True))
        block.sync(replay(SP))
    return nc


_CACHE = {}


def kernel(x_prompt, x_sample, cache_win_k, cache_win_v, rel_bias_table, norm_gains,
           ffn1_w_gate, ffn1_w_up, ffn1_w_down, w_in, attn_sinks, gmlp_ln_g, gmlp_ln_b,
           gmlp_w_s, gmlp_b_s, w_branch_attn, w_branch_gmlp, w_out,
           ffn2_w_gate, ffn2_w_up, ffn2_w_down):
    f = lambda a: np.ascontiguousarray(np.asarray(a, dtype=np.float32))
    if "nc" not in _CACHE:
        _CACHE["nc"] = build_program()
    nc = _CACHE["nc"]
    j = np.arange(384)
    bucket = t5_bucket_np(127 - j)
    oh = np.zeros((32, 384), np.float32)
    oh[bucket, j] = 1.0
    ident = np.eye(128, dtype=np.float32)
    jmat = np.zeros((128, 192), np.float32)
    for m in range(128):
        jmat[127 - m, m] = 1.0
    for m in range(16):
        jmat[127 - m, 128 + m] = 1.0
        jmat[127 - m, 128 + 32 + m] = 1.0
    shared = {
        "tbl": f(rel_bias_table), "gains": f(norm_gains[0]),
        "f1g": f(ffn1_w_gate[0]), "f1u": f(ffn1_w_up[0]), "f1d": f(ffn1_w_down[0]),
        "f2g": f(ffn2_w_gate[0]), "f2u": f(ffn2_w_up[0]), "f2d": f(ffn2_w_down[0]),
        "win": f(w_in[0]), "wba": f(w_branch_attn[0]), "wbg": f(w_branch_gmlp[0]), "wo": f(w_out[0]),
        "sinks": f(attn_sinks[0]).reshape(1, 16), "lng": f(gmlp_ln_g[0]).reshape(1, D), "lnb": f(gmlp_ln_b[0]).reshape(1, D),
        "wsd": f(gmlp_w_s[0]), "bsd": f(gmlp_b_s[0]).reshape(1, 512), "identd": ident, "ohd": oh, "jd": jmat,
    }
    xp = f(x_prompt)
    xs = f(x_sample)
    ckk = f(cache_win_k[0]).reshape(16, 128, 128)
    cvv = f(cache_win_v[0]).reshape(16, 128, 128)
    in_maps = []
    for c in range(8):
        m = dict(shared)
        m["xp"] = xp[c]
        m["xs"] = np.ascontiguousarray(xs[2 * c:2 * c + 2].reshape(32, D))
        m["ck"] = np.ascontiguousarray(ckk[2 * c:2 * c + 2])
        m["cv"] = np.ascontiguousarray(cvv[2 * c:2 * c + 2])
        in_maps.append(m)
    res = run_bass_kernel_spmd(nc, in_maps, core_ids=list(range(8)))
    R = res.results
    y_prompt = np.stack([np.asarray(r["yp"]) for r in R], 0).astype(np.float32)
    y_sample = np.concatenate([np.asarray(r["ys"]).reshape(2, 16, D) for r in R], 0).astype(np.float32)
    win_k = np.stack([np.asarray(r["wkp"]).reshape(128, 2, 64) for r in R], 0)[None].astype(np.float32)
    win_v = np.stack([np.asarray(r["wvp"]).reshape(128, 2, 64) for r in R], 0)[None].astype(np.float32)
    new_k = np.concatenate([np.asarray(r["nks"]).reshape(2, 16, 2, 64) for r in R], 0)[None].astype(np.float32)
    new_v = np.concatenate([np.asarray(r["nvs"]).reshape(2, 16, 2, 64) for r in R], 0)[None].astype(np.float32)
    gv = np.concatenate([np.asarray(r["gvs"]).reshape(2, 16, D) for r in R], 0)[None].astype(np.float32)
    return (y_prompt, y_sample, win_k, win_v, new_k, new_v, gv)
```
